# Optimizing a Trainium2 kernel written in Bass

```python
import math
import jax, jax.numpy as jnp
from jax import lax
import numpy as np

D_MODEL = 2048
BATCH = 16
SEQ = 256
DEPTH = 4
DEC_BATCH = 2
DEC_SEQ = 2048
PAST_LEN = 256

GRID_W = 64
EPS = 1e-6
ROPE_BASE = 10000.0
Q_BLOCK = 128

MLA_HEADS = 8
QK_NOPE = 128
QK_ROPE = 64
V_HEAD = 128
Q_LORA = 512
KV_LORA = 512
MLA_W = MLA_HEADS * V_HEAD
SGU_CHUNK = 128
SGU_GROUPS = 8
SGU_W = 1024
SGU_GC = SGU_W // SGU_GROUPS
RET_HEADS = 8
RET_DK = 128
RET_DV = 128
RET_W = RET_HEADS * RET_DV
RET_CHUNK = 128
N_BRANCH = 3

IN_SPLITS = (Q_LORA, KV_LORA, QK_ROPE, MLA_W,
             SGU_W, SGU_W, SGU_W,
             RET_HEADS * RET_DK, RET_HEADS * RET_DK, RET_W, RET_W,
             N_BRANCH * D_MODEL)
IN_COLS = sum(IN_SPLITS)

kernel_name = "hybrid_mla_sgu_retention_diffusion_step"


def rms_norm(x, g):
    xf = x.astype(jnp.float32)
    y = xf * lax.rsqrt(jnp.mean(xf * xf, axis=-1, keepdims=True) + EPS)
    return (y * g.astype(jnp.float32)).astype(x.dtype)


def axial_rope(n_tok):
    rows = n_tok // GRID_W
    row = jnp.repeat(jnp.arange(rows, dtype=jnp.float32), GRID_W)
    col = jnp.tile(jnp.arange(GRID_W, dtype=jnp.float32), rows)
    n_freq = QK_ROPE // 4
    inv = ROPE_BASE ** (-jnp.arange(n_freq, dtype=jnp.float32) / n_freq)
    ang_r = (row[:, None] * inv)[:, None, :]
    ang_c = (col[:, None] * inv)[:, None, :]
    return (jnp.cos(ang_r), jnp.sin(ang_r), jnp.cos(ang_c), jnp.sin(ang_c))


def rotate_pairs(x, cos, sin):
    x1, x2 = jnp.split(x, 2, axis=-1)
    return jnp.concatenate([x1 * cos - x2 * sin, x2 * cos + x1 * sin], axis=-1)


def apply_axial_rope(x, rope):
    cr, sr, cc, sc = rope
    xf = x.astype(jnp.float32)
    half = QK_ROPE // 2
    out = jnp.concatenate([rotate_pairs(xf[..., :half], cr, sr),
                           rotate_pairs(xf[..., half:], cc, sc)], axis=-1)
    return out.astype(x.dtype)


def block_attention(q, k, v):
    b, lq, h, dk = q.shape
    scale = dk ** -0.5
    qb = q.reshape(b, lq // Q_BLOCK, Q_BLOCK, h, dk).transpose(1, 0, 2, 3, 4)

    def one(qblk):
        s = jnp.einsum('bqhd,bkhd->bhqk', qblk, k).astype(jnp.float32) * scale
        p = jax.nn.softmax(s, axis=-1).astype(v.dtype)
        return jnp.einsum('bhqk,bkhe->bqhe', p, v)

    o = lax.map(one, qb)
    return o.transpose(1, 0, 2, 3, 4).reshape(b, lq, h, v.shape[-1])


def mla_query(cq, g_q, w_uq, rope):
    b, L, _ = cq.shape
    q = (rms_norm(cq, g_q) @ w_uq).reshape(b, L, MLA_HEADS, QK_NOPE + QK_ROPE)
    if rope is not None:
        q = jnp.concatenate([q[..., :QK_NOPE], apply_axial_rope(q[..., QK_NOPE:], rope)], axis=-1)
    return q


def mla_keys(ckv_n, k_rope, w_ukv):
    b, L, _ = ckv_n.shape
    kv = (ckv_n @ w_ukv).reshape(b, L, MLA_HEADS, QK_NOPE + V_HEAD)
    k = jnp.concatenate([kv[..., :QK_NOPE],
                         jnp.broadcast_to(k_rope[:, :, None, :], (b, L, MLA_HEADS, QK_ROPE))], axis=-1)
    return k, kv[..., QK_NOPE:]


def chunk_sgu(u, v, g_sgu, w_s, b_s):
    b, L, _ = u.shape
    vn = rms_norm(v, g_sgu).reshape(b, L // SGU_CHUNK, SGU_CHUNK, SGU_GROUPS, SGU_GC)
    s = jnp.einsum('gpq,bnqgc->bnpgc', w_s, vn) + b_s[None, None, :, :, None]
    return u * s.reshape(b, L, SGU_W).astype(u.dtype)


def retention_scan(q, k, v, log_g, r0, inclusive):
    b, L, h, _ = q.shape
    C = RET_CHUNK
    n = L // C
    idx = jnp.arange(C, dtype=jnp.float32)
    diff = idx[:, None] - idx[None, :]
    mask = (diff >= 0) if inclusive else (diff > 0)
    dec = jnp.where(mask[None], jnp.exp(jnp.where(mask, diff, 0.0)[None] * log_g[:, None, None]), 0.0)
    xi = jnp.exp((idx[:, None] + 1.0) * log_g[None, :])[None, :, :, None]
    zeta = jnp.exp((C - 1.0 - idx)[:, None] * log_g[None, :])[None, :, :, None]
    g_c = jnp.exp(C * log_g)[None, :, None, None]

    def to_chunks(t):
        return t.reshape(b, n, C, h, t.shape[-1]).transpose(1, 0, 2, 3, 4)

    def step(r, inp):
        qc, kc, vc = inp
        att = jnp.einsum('bihd,bjhd->bhij', qc, kc) * dec[None]
        o = jnp.einsum('bhij,bjhe->bihe', att, vc) + jnp.einsum('bihd,bhde->bihe', qc, r) * xi
        r = g_c * r + jnp.einsum('bjhd,bjhe->bhde', kc * zeta, vc)
        return r, o

    r, o = lax.scan(step, r0, (to_chunks(q), to_chunks(k), to_chunks(v)))
    return o.transpose(1, 0, 2, 3, 4).reshape(b, L, h, v.shape[-1]), r


def bidir_retention(q, k, v, log_g2, r0_f, r0_b):
    o_f, r_f = retention_scan(q, k, v, log_g2[0], r0_f, True)
    flip = lambda t: jnp.flip(t, axis=1)
    o_b, r_b = retention_scan(flip(q), flip(k), flip(v), log_g2[1], r0_b, False)
    return o_f + flip(o_b), r_f, r_b


def head_layer_norm(o, g):
    b, L = o.shape[:2]
    mu = jnp.mean(o, axis=-1, keepdims=True)
    var = jnp.mean(jnp.square(o - mu), axis=-1, keepdims=True)
    y = (o - mu) * lax.rsqrt(var + EPS)
    return y.reshape(b, L, RET_W) * g.astype(jnp.float32)


def layer_forward(x, cond, lw, rope, ctx):
    b, L, _ = x.shape
    mod = jax.nn.silu(cond) @ lw['w_mod'] + lw['b_mod']
    shift, scale, gate = jnp.split(mod[:, None, :], 3, axis=-1)
    h = rms_norm(x, lw['g_pre']) * (1 + scale) + shift
    z = h @ lw['w_in']
    offs = tuple(int(o) for o in np.cumsum(IN_SPLITS)[:-1])
    (cq, ckv_raw, krope, gp_mla, u, v_s, gp_sgu, rq, rk, rv, gp_ret, merge) = jnp.split(z, offs, axis=-1)

    ckv_n = rms_norm(ckv_raw, lw['g_kv'])
    if rope is not None:
        krope = apply_axial_rope(krope[:, :, None, :], rope)[:, :, 0, :]
    q = mla_query(cq, lw['g_q'], lw['w_uq'], rope)
    k, vv = mla_keys(ckv_n, krope, lw['w_ukv'])
    if ctx is not None:
        kc, vc = mla_keys(ctx[0], ctx[1], lw['w_ukv'])
        k = jnp.concatenate([kc, k], axis=1)
        vv = jnp.concatenate([vc, vv], axis=1)
    o_mla = block_attention(q, k, vv).reshape(b, L, MLA_W)

    o_sgu = chunk_sgu(u, v_s, lw['g_sgu'], lw['w_sgu'], lw['b_sgu'])

    log_g2 = jax.nn.log_sigmoid(lw['ret_decay'].astype(jnp.float32))
    rqf = rq.reshape(b, L, RET_HEADS, RET_DK).astype(jnp.float32)
    rkf = rk.reshape(b, L, RET_HEADS, RET_DK).astype(jnp.float32) * (RET_DK ** -0.5)
    rvf = rv.reshape(b, L, RET_HEADS, RET_DV).astype(jnp.float32)
    if ctx is None:
        r0_f = jnp.zeros((b, RET_HEADS, RET_DK, RET_DV), jnp.float32)
        r0_b = r0_f
    else:
        r0_f = ctx[2].astype(jnp.float32)
        r0_b = ctx[3].astype(jnp.float32)
    o_ret, r_f, r_b = bidir_retention(rqf, rkf, rvf, log_g2, r0_f, r0_b)
    o_ret = head_layer_norm(o_ret, lw['g_ret']).astype(x.dtype)

    y_mla = (jax.nn.silu(gp_mla) * o_mla) @ lw['w_br_mla']
    y_sgu = (jax.nn.silu(gp_sgu) * o_sgu) @ lw['w_br_sgu']
    y_ret = (jax.nn.silu(gp_ret) * o_ret) @ lw['w_br_ret']
    m_mla, m_sgu, m_ret = jnp.split(jax.nn.sigmoid(merge), N_BRANCH, axis=-1)
    y = (m_mla * y_mla + m_sgu * y_sgu + m_ret * y_ret) @ lw['w_out']
    x = x + gate * rms_norm(y, lw['g_post'])
    return x, (ckv_n, krope, r_f.astype(x.dtype), r_b.astype(x.dtype))


def setup_inputs(seed: int = 0) -> dict:
    key = jax.random.key(seed)
    ks = jax.random.split(key, 25)
    f32 = jnp.float32
    nrm = lambda k, shape, s: jax.random.normal(k, shape, f32) * s
    heads = jnp.arange(RET_HEADS, dtype=f32)
    one_minus = 2.0 ** (-5.0 - heads)
    decay_logit = jnp.log1p(-one_minus) - jnp.log(one_minus)
    return {
        "x_prompt": nrm(ks[0], (BATCH, SEQ, D_MODEL), 1.0),
        "x_sample": nrm(ks[1], (DEC_BATCH, DEC_SEQ, D_MODEL), 1.0),
        "cache_ckv": nrm(ks[2], (DEC_BATCH, DEPTH, PAST_LEN, KV_LORA), 1.0),
        "cache_krope": nrm(ks[3], (DEC_BATCH, DEPTH, PAST_LEN, QK_ROPE), 1.0),
        "state_ret": nrm(ks[4], (DEC_BATCH, DEPTH, 2, RET_HEADS, RET_DK, RET_DV), 0.5),
        "c": nrm(ks[5], (DEC_BATCH, D_MODEL), 1.0),
        "c_ctx": nrm(ks[6], (D_MODEL,), 1.0),
        "w_mod": nrm(ks[7], (DEPTH, D_MODEL, 3 * D_MODEL), 0.5 * D_MODEL ** -0.5),
        "b_mod": nrm(ks[8], (DEPTH, 3 * D_MODEL), 0.01),
        "g_pre": 1.0 + nrm(ks[9], (DEPTH, D_MODEL), 0.1),
        "g_post": 1.0 + nrm(ks[10], (DEPTH, D_MODEL), 0.1),
        "w_in": nrm(ks[11], (DEPTH, D_MODEL, IN_COLS), D_MODEL ** -0.5),
        "g_q": 1.0 + nrm(ks[12], (DEPTH, Q_LORA), 0.1),
        "g_kv": 1.0 + nrm(ks[13], (DEPTH, KV_LORA), 0.1),
        "w_uq": nrm(ks[14], (DEPTH, Q_LORA, MLA_HEADS * (QK_NOPE + QK_ROPE)), Q_LORA ** -0.5),
        "w_ukv": nrm(ks[15], (DEPTH, KV_LORA, MLA_HEADS * (QK_NOPE + V_HEAD)), KV_LORA ** -0.5),
        "g_sgu": 1.0 + nrm(ks[16], (DEPTH, SGU_W), 0.1),
        "w_sgu": nrm(ks[17], (DEPTH, SGU_GROUPS, SGU_CHUNK, SGU_CHUNK), SGU_CHUNK ** -0.5),
        "b_sgu": nrm(ks[18], (DEPTH, SGU_CHUNK, SGU_GROUPS), 0.1),
        "ret_decay": decay_logit[None, None, :] + nrm(ks[19], (DEPTH, 2, RET_HEADS), 0.1),
        "g_ret": 1.0 + nrm(ks[20], (DEPTH, RET_W), 0.1),
        "w_br_mla": nrm(ks[21], (DEPTH, MLA_W, D_MODEL), MLA_W ** -0.5),
        "w_br_sgu": nrm(ks[22], (DEPTH, SGU_W, D_MODEL), SGU_W ** -0.5),
        "w_br_ret": nrm(ks[23], (DEPTH, RET_W, D_MODEL), RET_W ** -0.5),
        "w_out": nrm(ks[24], (DEPTH, D_MODEL, D_MODEL), D_MODEL ** -0.5),
    }


def reference(x_prompt, x_sample, cache_ckv, cache_krope, state_ret, c, c_ctx,
              w_mod, b_mod, g_pre, g_post, w_in, g_q, g_kv, w_uq, w_ukv,
              g_sgu, w_sgu, b_sgu, ret_decay, g_ret, w_br_mla, w_br_sgu, w_br_ret, w_out):
    def layer_weights(l):
        return dict(w_mod=w_mod[l], b_mod=b_mod[l], g_pre=g_pre[l], g_post=g_post[l], w_in=w_in[l],
                    g_q=g_q[l], g_kv=g_kv[l], w_uq=w_uq[l], w_ukv=w_ukv[l], g_sgu=g_sgu[l],
                    w_sgu=w_sgu[l], b_sgu=b_sgu[l], ret_decay=ret_decay[l], g_ret=g_ret[l],
                    w_br_mla=w_br_mla[l], w_br_sgu=w_br_sgu[l], w_br_ret=w_br_ret[l], w_out=w_out[l])

    y_prompt = x_prompt
    ckvs, kropes, rets = [], [], []
    for l in range(DEPTH):
        y_prompt, (ckv_n, kr, r_f, r_b) = layer_forward(y_prompt, c_ctx[None, :], layer_weights(l), None, None)
        ckvs.append(ckv_n)
        kropes.append(kr)
        rets.append(jnp.stack([r_f, r_b], axis=1))
    new_ckv = jnp.stack(ckvs, axis=1)
    new_krope = jnp.stack(kropes, axis=1)
    new_ret = jnp.stack(rets, axis=1)

    rope = axial_rope(x_sample.shape[1])
    y_sample = x_sample
    for l in range(DEPTH):
        ctx = (cache_ckv[:, l], cache_krope[:, l], state_ret[:, l, 0], state_ret[:, l, 1])
        y_sample, _ = layer_forward(y_sample, c, layer_weights(l), rope, ctx)

    return (y_prompt, y_sample, new_ckv, new_krope, new_ret)
```

```python
import contextlib
import os
import numpy as np
import concourse.bass as bass
import concourse.mybir as mybir
from concourse.bass_utils import run_bass_kernel_spmd

F32, BF16 = mybir.dt.float32, mybir.dt.bfloat16
AF = mybir.ActivationFunctionType
ALU = mybir.AluOpType
AX = mybir.AxisListType
EPS = 1e-6
DM, INC = 2048, 15424
C_CQ, C_CKV, C_KR, C_GPM, C_U, C_VS, C_GPS, C_RQ, C_GPR, C_MRG = 0, 512, 1024, 1088, 2112, 3136, 4160, 5184, 8256, 9280
BIG = 1.0e5
ENG = ['pe', 'act', 'dve', 'pool', 'sp']
KSEM = 6


class StopBuild(Exception):
    pass


def _stop(tag):
    if os.environ.get('K_STOP') == tag:
        raise StopBuild()


class Op:
    __slots__ = ('eng', 'fn', 'dma', 'sig', 'pos', 'deps', 'dsem', 'dval', 'cnt')


class Prog:
    def __init__(s):
        s.q = {e: [] for e in ENG}
        s.lastw, s.rd = {}, {}
        s.pend = {e: set() for e in ENG}
        s.dcount = {}
        s.rr = {e: 0 for e in ENG}
        s.dmas = []

    def add(s, eng, fn, r=(), w=(), dma=False):
        op = Op()
        op.eng, op.fn, op.dma, op.sig, op.pos = eng, fn, dma, False, len(s.q[eng])
        x_ = [('x',) + k for k in r if isinstance(k, tuple) and k[0] in ('pb', 'pt')]
        if x_:
            w = list(w) + x_
        deps = s.pend[eng]
        s.pend[eng] = set()
        for k in r:
            x = s.lastw.get(k)
            if x is not None:
                deps.add(x)
        for k in w:
            x = s.lastw.get(k)
            if x is not None:
                deps.add(x)
            d = s.rd.get(k)
            if d:
                for y in d.values():
                    if isinstance(y, list):
                        deps.update(y)
                    else:
                        deps.add(y)
        for k in r:
            d = s.rd.setdefault(k, {})
            if dma:
                d.setdefault('dma', []).append(op)
            else:
                d[eng] = op
        for k in w:
            s.lastw[k] = op
            s.rd[k] = {}
        deps.discard(op)
        op.deps = deps
        if dma:
            sem = (eng, s.rr[eng] % KSEM)
            s.rr[eng] += 1
            s.dcount[sem] = s.dcount.get(sem, 0) + 1
            op.dsem, op.dval = sem, 16 * s.dcount[sem]
            s.dmas.append(op)
        s.q[eng].append(op)
        return op

    def barrier(s):
        last = set(s.dmas)
        s.dmas = []
        for e in ENG:
            for o in reversed(s.q[e]):
                if not o.dma and o.fn is not None:
                    last.add(o)
                    break
        for e in ENG:
            s.pend[e] |= last

    def emit(s, nc):
        for e in ENG:
            if s.pend[e]:
                s.add(e, None)
        for e in ENG:
            for op in s.q[e]:
                for d in op.deps:
                    if not d.dma:
                        if d.eng == 'pe' and op.eng == 'pe':
                            continue
                        d.sig = True
        for e in ENG:
            c = 0
            for op in s.q[e]:
                if op.sig:
                    c += 1
                op.cnt = c
        with contextlib.ExitStack() as es:
            csem = {e: es.enter_context(nc.semaphore("c_" + e)) for e in ENG if e != 'sp'}
            dsem = {k: es.enter_context(nc.semaphore("d_%s%d" % k)) for k in s.dcount}
            block = es.enter_context(nc.Block())

            def run(e, eng):
                seen = {}

                def wait(sem, key, val):
                    if val > 0 and seen.get(key, 0) < val:
                        eng.wait_ge(sem, val)
                        seen[key] = val
                for op in s.q[e]:
                    for d in op.deps:
                        if d.dma:
                            wait(dsem[d.dsem], d.dsem, d.dval)
                        elif not (d.eng == 'pe' and e == 'pe'):
                            wait(csem[d.eng], d.eng, d.cnt)
                    if op.fn is None:
                        continue
                    if op.dma:
                        wait(dsem[op.dsem], op.dsem, op.dval - 16)
                        op.fn(eng).then_inc(dsem[op.dsem], 16)
                    else:
                        ins = op.fn(eng)
                        if op.sig:
                            ins.then_inc(csem[e], 1)
            block.tensor(lambda eng: run('pe', eng))
            block.scalar(lambda eng: run('act', eng))
            block.vector(lambda eng: run('dve', eng))
            block.gpsimd(lambda eng: run('pool', eng))
            block.sync(lambda eng: run('sp', eng))


class Arena:
    def __init__(s, nc, base=18432, top=229344):
        s.nc, s.off, s.top, s.n = nc, base, top, 0

    def alloc(s, shape, dt):
        nb = int(np.prod(shape[1:])) * (2 if dt == BF16 else 4)
        nb = (nb + 31) // 32 * 32
        h = s.nc.alloc_sbuf_tensor_at("t%d" % s.n, list(shape), dt, offset=s.off)
        s.n += 1
        s.off += nb
        assert s.off <= s.top, ("SBUF overflow", s.off)
        return h


def blocks_of(n, w=512):
    return [(b, min(w, n - b)) for b in range(0, n, w)]


def build(NL, NCH):
    T, NK = NCH * 128, NCH + 2
    TK = NK * 128
    nc = bass.Bass("TRN2", target_bir_lowering=False)
    P = Prog()
    D = {}

    def din(n, shape):
        D[n] = nc.dram_tensor(n, list(shape), F32, kind="ExternalInput").ap()

    def dout(n, shape):
        D[n] = nc.dram_tensor(n, list(shape), F32, kind="ExternalOutput").ap()

    def dscr(n, shape, dt):
        D[n] = nc.dram_tensor(n, list(shape), dt, kind="Internal").ap()
    din("xin", [T, DM]); din("cond", [128, 16]); din("cckv", [NL, 256, 512]); din("ckr", [NL, 256, 64])
    din("r0", [NL, 2, 8, 128, 128]); din("ropeC", [T, 64]); din("ropeS", [T, 64])
    din("qmask", [32, T]); din("kmask", [32, TK]); din("carry", [128, 2 * NCH])
    din("dexf", [128, 128]); din("dexb", [128, 128]); din("xiexp", [128, 256]); din("zexp", [128, 2])
    din("ident", [128, 128])
    din("w_mod", [NL, DM, 3 * DM]); din("b_mod", [NL, 3 * DM]); din("g_pre", [NL, DM]); din("g_post", [NL, DM])
    din("w_in", [NL, DM, INC]); din("g_q", [NL, 512]); din("g_kv", [NL, 512]); din("w_uq", [NL, 512, 1536])
    din("w_ukv", [NL, 512, 2048]); din("g_sgu", [NL, 1024]); din("wsT", [NL, 128, 1024]); din("bsT", [NL, 1024])
    din("ret_decay", [NL, 16]); din("g_ret", [NL, 1024]); din("w_br", [NL, 3, 1024, DM]); din("w_out", [NL, DM, DM])
    dout("y", [T, DM]); dout("o_ckv", [NL, T, 512]); dout("o_kr", [NL, T, 64])
    dout("o_ret", [NL, 2, max(NCH // 2, 1), 8, 128, 128])
    dscr("ztm", [T, 5120], BF16)
    dscr("zT", [9216, T], BF16)
    dscr("gT", [3, 1024, T], BF16)
    dscr("ysT", [DM, T], BF16)
    dscr("bst", [NCH, 128, 1024], BF16)
    A = Arena(nc)
    pbf = [nc.alloc_psum_tensor("pb%d" % i, [128, 512], F32) for i in range(6)]
    pbt = [nc.alloc_psum_tensor("pt%d" % i, [128, 1024], BF16) for i in range(2)]
    cnt = {'fb': 0, 'tb': 0, 'st': 0, 'ev': 0}

    def fb():
        cnt['fb'] += 1
        i = cnt['fb'] % 6
        return pbf[i], ('pb', i)

    def fb4():
        cnt['fb'] += 1
        i = cnt['fb'] % 4
        return pbf[i], ('pb', i)

    def tbk():
        cnt['tb'] += 1
        i = cnt['tb'] % 2
        return pbt[i], ('pt', i)

    def mm(out, lhsT, rhs, start, stop, r, w):
        P.add('pe', lambda e: e.matmul(out, lhsT, rhs, start=start, stop=stop), r, w)

    def tr(out, in_, r, w):
        P.add('pe', lambda e: e.transpose(out, in_, ident[0:in_.shape[0], 0:in_.shape[0]]), list(r) + ['ident'], w)

    def act(out, in_, func, r, w, scale=1.0, bias=0.0, accum=None):
        if accum is None:
            P.add('act', lambda e: e.activation(out=out, in_=in_, func=func, scale=scale, bias=bias), r, w)
        else:
            P.add('act', lambda e: e.activation(out=out, in_=in_, func=func, scale=scale, bias=bias, accum_out=accum), r, w)

    def tt(eng, out, in0, in1, op, r, w):
        P.add(eng, lambda e: e.tensor_tensor(out=out, in0=in0, in1=in1, op=op), r, w)

    def stt(out, in0, scalar, in1, op0, op1, r, w, eng='dve'):
        P.add(eng, lambda e: e.scalar_tensor_tensor(out=out, in0=in0, scalar=scalar, in1=in1, op0=op0, op1=op1), r, w)

    def ts(out, in0, s1, s2, op0, op1, r, w, eng='dve'):
        if s2 is None:
            P.add(eng, lambda e: e.tensor_scalar(out=out, in0=in0, scalar1=s1, scalar2=None, op0=op0), r, w)
        else:
            P.add(eng, lambda e: e.tensor_scalar(out=out, in0=in0, scalar1=s1, scalar2=s2, op0=op0, op1=op1), r, w)

    def cp(eng, out, in_, r, w):
        if eng == 'act':
            P.add('act', lambda e: e.copy(out=out, in_=in_), r, w)
        else:
            P.add(eng, lambda e: e.tensor_copy(out=out, in_=in_), r, w)

    def evcp(out, in_, r, w):
        cnt['ev'] += 1
        cp('act' if cnt['ev'] % 2 else 'dve', out, in_, r, w)

    def recip(out, in_, r, w):
        P.add('dve', lambda e: e.reciprocal(out=out, in_=in_), r, w)

    def dma(eng, out, in_, r, w):
        P.add(eng, lambda e: e.dma_start(out=out, in_=in_), r, w, dma=True)

    def rstd_of(ssq, n, rk):
        t = ssq.tensor
        return None

    ident = A.alloc([128, 128], BF16)
    ones = A.alloc([128, 128], BF16)
    ropeC = A.alloc([128, NCH, 64], F32)
    ropeS = A.alloc([128, NCH, 64], F32)
    csrep = A.alloc([128, 16, 128], BF16)
    cs = A.alloc([128, 16], BF16)
    condt = A.alloc([128, 16], F32)
    carry = A.alloc([128, 2 * NCH], F32)
    kropeT = A.alloc([128, TK], BF16)
    stt_ring = [A.alloc([128, 8], F32) for _ in range(8)]
    dexf = A.alloc([128, 128], F32)
    dexb = A.alloc([128, 128], F32)
    xiexp = A.alloc([128, 256], F32)
    zexp = A.alloc([128, 2], F32)
    stag = [A.alloc([128, 512], BF16) for _ in range(4)]
    base_mark = A.off

    def st_new():
        cnt['st'] += 1
        i = cnt['st'] % 8
        return stt_ring[i], ('st', i)
    sg = {'i': 0}

    def stag_new():
        sg['i'] += 1
        i = sg['i'] % 4
        return stag[i], ('stag', i)

    dma('pool', ident[:], D["ident"][:, :], [], ['ident'])
    P.add('dve', lambda e: e.memset(ones[:], 1.0), [], ['ones'])
    dma('sp', ropeC[:], D["ropeC"].rearrange("(c p) d -> p c d", p=128), [], ['ropeC'])
    dma('sp', ropeS[:], D["ropeS"].rearrange("(c p) d -> p c d", p=128), [], ['ropeS'])
    dma('sp', condt[:], D["cond"][:, :], [], ['condt'])
    dma('sp', carry[:], D["carry"][:, :], [], ['carry'])
    dma('sp', dexf[:], D["dexf"][:, :], [], ['dexf'])
    dma('sp', dexb[:], D["dexb"][:, :], [], ['dexb'])
    dma('sp', xiexp[:], D["xiexp"][:, :], [], ['xiexp'])
    dma('sp', zexp[:], D["zexp"][:, :], [], ['zexp'])
    P.add('dve', lambda e: e.memset(kropeT[:], 0.0), [], ['kmaskrows'] + [('kropeT', j) for j in range(NK)])
    dma('pool', kropeT[64:96, :], D["kmask"][:, :], [], ['kmaskrows'])
    act(cs[:], condt[:], AF.Silu, ['condt'], ['cs'])
    cp('dve', csrep[:], cs[:].unsqueeze(2).broadcast_to([128, 16, 128]), ['cs'], ['csrep'])

    def norm_stat(bank_ap, bk, n, junk_ap, jk):
        st, sk = st_new()
        act(junk_ap, bank_ap, AF.Square, [bk], [jk, sk], accum=st[:, 0:1])
        act(st[:, 1:2], st[:, 0:1], AF.Sqrt, [sk], [sk], scale=1.0 / n, bias=EPS)
        recip(st[:, 2:3], st[:, 1:2], [sk], [sk])
        return st[:, 2:3], sk

    def layer(l):
        xsrc = D["xin"] if l == 0 else D["y"]
        xkey = 'xin' if l == 0 else 'y'
        A.off = base_mark
        cqnT = A.alloc([128, 4, T], BF16)
        ckvnT = A.alloc([128, 4, TK], BF16)
        att_mark = A.off
        hT = A.alloc([128, 16, T], BF16)
        wt = [A.alloc([128, 16, 512], BF16) for _ in range(2)]
        Ab = A.alloc([128, DM], F32)
        Bb = A.alloc([128, DM], F32)
        xc = [A.alloc([128, DM], F32) for _ in range(2)]
        hbs = [A.alloc([128, DM], BF16) for _ in range(2)]
        gqb = A.alloc([128, 512], F32)
        gkvb = A.alloc([128, 512], F32)
        tf = [A.alloc([128, 512], F32) for _ in range(2)]
        tb = [A.alloc([128, 512], BF16) for _ in range(2)]
        r1 = A.alloc([128, 64], F32)
        wi = {'i': 0}

        def wt_new():
            wi['i'] += 1
            i = wi['i'] % 2
            return wt[i], ('wt', i)

        def mod_third(j, dst, dk):
            dma('sp', dst[:], D["b_mod"][l:l + 1, j * DM:(j + 1) * DM].partition_broadcast(128), [], [dk])
            for n4 in range(4):
                w_, wk = wt_new()
                c0 = j * DM + n4 * 512
                dma('pool', w_[:], D["w_mod"][l, :, c0:c0 + 512].rearrange("(kc p) n -> p kc n", p=128), [], [wk])
                bk_, bkk = fb()
                for kc in range(16):
                    mm(bk_[:, :], csrep[:, kc, :], w_[:, kc, :], kc == 0, kc == 15, [wk, 'csrep'], [bkk])
                tt('dve', dst[:, n4 * 512:(n4 + 1) * 512], bk_[:, :], dst[:, n4 * 512:(n4 + 1) * 512], ALU.add, [bkk, dk], [dk])

        dma('sp', gqb[:], D["g_q"][l:l + 1, :].partition_broadcast(128), [], ['gqb'])
        dma('sp', gkvb[:], D["g_kv"][l:l + 1, :].partition_broadcast(128), [], ['gkvb'])
        _stop('PRE')
        mod_third(0, Bb, 'Bb')
        mod_third(1, Ab, 'Ab')
        dma('sp', xc[0][:], D["g_pre"][l:l + 1, :].partition_broadcast(128), [], [('xc', 0)])
        stt(Ab[:], Ab[:], 1.0, xc[0][:], ALU.add, ALU.mult, ['Ab', ('xc', 0)], ['Ab'])
        _stop('MOD')
        dma('sp', xc[0][:], xsrc[0:128, :], [(xkey, 0)], [('xc', 0)])
        for c in range(NCH):
            x_, xk = xc[c % 2], ('xc', c % 2)
            hb, hbk = hbs[c % 2], ('hb', c % 2)
            if c + 1 < NCH:
                dma('sp', xc[(c + 1) % 2][:], xsrc[(c + 1) * 128:(c + 2) * 128, :], [(xkey, c + 1)], [('xc', (c + 1) % 2)])
            rs, sk = norm_stat(x_[:], xk, DM, hb[:], hbk)
            stt(x_[:], x_[:], rs, Ab[:], ALU.mult, ALU.mult, [xk, sk, 'Ab'], [xk])
            tt('pool', hb[:], x_[:], Bb[:], ALU.add, [xk, 'Bb'], [hbk])
            for k4 in range(4):
                t_, tk = tbk()
                for j in range(4):
                    kc = k4 * 4 + j
                    tr(t_[:, j * 128:(j + 1) * 128], hb[:, kc * 128:(kc + 1) * 128], [hbk], [tk])
                evcp(hT[:, k4 * 4:k4 * 4 + 4, c * 128:(c + 1) * 128], t_[:, 0:512].rearrange("p (j t) -> p j t", j=4), [tk], [('hT', c)])
        _stop('H')
        for j in range(2):
            b_, bk_ = tb[j], ('tb', j)
            dma('pool', b_[:], D["cckv"][l, j * 128:(j + 1) * 128, :], [], [bk_])
            t_, tk = tbk()
            for q4 in range(4):
                tr(t_[:, q4 * 128:(q4 + 1) * 128], b_[:, q4 * 128:(q4 + 1) * 128], [bk_], [tk])
            evcp(ckvnT[:, :, j * 128:(j + 1) * 128], t_[:, 0:512].rearrange("p (j t) -> p j t", j=4), [tk], [('ckvnT', j)])
            s_, sk_ = stag_new()
            dma('pool', s_[:, 0:64], D["ckr"][l, j * 128:(j + 1) * 128, :], [], [sk_])
            t_, tk = tbk()
            tr(t_[0:64, 0:128], s_[:, 0:64], [sk_], [tk])
            cp('act', kropeT[0:64, j * 128:(j + 1) * 128], t_[0:64, 0:128], [tk], [('kropeT', j)])

        _stop('CTX')
        def tm_pass(c0, ncols, evac):
            w_, wk = wt_new()
            dma('pool', w_[:, :, 0:ncols], D["w_in"][l, :, c0:c0 + ncols].rearrange("(kc p) n -> p kc n", p=128), [], [wk])
            for c in range(NCH):
                bk_, bkk = fb()
                for kc in range(16):
                    mm(bk_[:, 0:ncols], hT[:, kc, c * 128:(c + 1) * 128], w_[:, kc, 0:ncols], kc == 0, kc == 15, [wk, ('hT', c)], [bkk])
                evac(c, bk_, bkk)

        def normed(c, bk_, bkk, gb, gk, outT, okey, coff, fp32_out):
            i = c % 2
            rs, sk = norm_stat(bk_[:, :], bkk, 512, tf[i][:], ('tf', i))
            if fp32_out is not None:
                stt(tf[i][:], bk_[:, :], rs, gb[:], ALU.mult, ALU.mult, [bkk, sk, gk], [('tf', i)])
                dma('sp', fp32_out[c * 128:(c + 1) * 128, :], tf[i][:], [('tf', i)], [])
                cp('pool', tb[i][:], tf[i][:], [('tf', i)], [('tb', i)])
            else:
                stt(tb[i][:], bk_[:, :], rs, gb[:], ALU.mult, ALU.mult, [bkk, sk, gk], [('tb', i)])
            t_, tk = tbk()
            for q4 in range(4):
                tr(t_[:, q4 * 128:(q4 + 1) * 128], tb[i][:, q4 * 128:(q4 + 1) * 128], [('tb', i)], [tk])
            evcp(outT[:, :, coff + c * 128:coff + (c + 1) * 128], t_[:, 0:512].rearrange("p (j t) -> p j t", j=4), [tk], [(okey, coff // 128 + c)])

        def rope(src, sk, c, dst, dk, nh, slot=0):
            Cv = ropeC[:, c, :].unsqueeze(1).broadcast_to([128, nh, 64])
            RT1, RT2 = RTs[slot]
            k1, k2 = ('RT1', slot), ('RT2', slot)
            tt('dve', RT1[:, 0:nh, :], src, Cv, ALU.mult, [sk, 'ropeC'], [k1])
            s4 = src.rearrange("p n (a h d) -> p n a h d", a=2, h=2)
            o4 = RT2[:, 0:nh, :].rearrange("p n (a h d) -> p n a h d", a=2, h=2)
            S4 = ropeS[:, c, :].rearrange("p (a h d) -> p a h d", a=2, h=2)
            for hh in range(2):
                for a in range(2):
                    tt('dve', o4[:, :, a, hh, :], s4[:, :, a, 1 - hh, :], S4[:, a, hh, :].unsqueeze(1).broadcast_to([128, nh, 16]), ALU.mult, [sk, 'ropeS'], [k2])
            tt('dve', dst, RT1[:, 0:nh, :], RT2[:, 0:nh, :], ALU.add, [k1, k2], [dk])

        RTs = [(A.alloc([128, 1, 64], F32), A.alloc([128, 1, 64], F32)) for _ in range(2)]

        def ev_kr(c, bk_, bkk):
            rope(bk_[:, 0:64].rearrange("p (n d) -> p n d", n=1), bkk, c, r1[:].rearrange("p (n d) -> p n d", n=1), 'r1', 1)
            dma('sp', D["o_kr"][l, c * 128:(c + 1) * 128, :], r1[:], ['r1'], [])
            s_, sk_ = stag_new()
            cp('pool', s_[:, 0:64], r1[:], ['r1'], [sk_])
            t_, tk = tbk()
            tr(t_[0:64, 0:128], s_[:, 0:64], [sk_], [tk])
            cp('act', kropeT[0:64, (2 + c) * 128:(3 + c) * 128], t_[0:64, 0:128], [tk], [('kropeT', 2 + c)])

        def ev_plain(col0, scale=1.0, func=None):
            def f(c, bk_, bkk):
                s_, sk_ = stag_new()
                if func is not None:
                    act(s_[:], bk_[:, :], func, [bkk], [sk_])
                elif scale != 1.0:
                    act(s_[:], bk_[:, :], AF.Copy, [bkk], [sk_], scale=scale)
                else:
                    cp('dve', s_[:], bk_[:, :], [bkk], [sk_])
                dma('sp', D["ztm"][c * 128:(c + 1) * 128, col0:col0 + 512], s_[:], [sk_], [('ztm', c, col0 // 512)])
            return f

        tm_pass(C_CQ, 512, lambda c, b, k: normed(c, b, k, gqb, 'gqb', cqnT, 'cqnT', 0, None))
        tm_pass(C_CKV, 512, lambda c, b, k: normed(c, b, k, gkvb, 'gkvb', ckvnT, 'ckvnT', 256, D["o_ckv"][l]))
        tm_pass(C_KR, 64, ev_kr)
        _stop('ZA')
        for i in range(2):
            tm_pass(C_VS + i * 512, 512, ev_plain(i * 512))
        for i in range(8):
            col = C_RQ + i * 512
            sc = (128 ** -0.5) if 2 <= i < 4 else 1.0
            tm_pass(col, 512, ev_plain(1024 + i * 512, scale=sc, func=AF.Silu if i >= 6 else None))

        def fm_pass(c0, func, row0):
            w_, wk = wt_new()
            dma('pool', w_[:], D["w_in"][l, :, c0:c0 + 512].rearrange("(kc p) n -> p kc n", p=128), [], [wk])
            for sub in range(4):
                for bi, (b0, bw) in enumerate(blocks_of(T)):
                    bk_, bkk = fb()
                    for kc in range(16):
                        mm(bk_[:, 0:bw], w_[:, kc, sub * 128:(sub + 1) * 128], hT[:, kc, b0:b0 + bw], kc == 0, kc == 15,
                           [wk] + [('hT', c) for c in range(b0 // 128, (b0 + bw) // 128)], [bkk])
                    s_, sk_ = stag_new()
                    if func is None:
                        cp('dve', s_[:, 0:bw], bk_[:, 0:bw], [bkk], [sk_])
                    else:
                        act(s_[:, 0:bw], bk_[:, 0:bw], func, [bkk], [sk_])
                    rr = row0 + sub * 128
                    dma('sp', D["zT"][rr:rr + 128, b0:b0 + bw], s_[:, 0:bw], [sk_], [('zT', rr // 128, bi)])
        for i in range(2):
            fm_pass(C_GPM + i * 512, AF.Silu, i * 512)
        for i in range(2):
            fm_pass(C_U + i * 512, None, 1024 + i * 512)
        for i in range(2):
            fm_pass(C_GPS + i * 512, AF.Silu, 2048 + i * 512)
        for i in range(12):
            fm_pass(C_MRG + i * 512, AF.Sigmoid, 3072 + i * 512)
        P.barrier()

        _stop('Z')
        A.off = att_mark
        wuq = A.alloc([128, 4, 1536], BF16)
        wukv = A.alloc([128, 4, 2048], BF16)
        qb = [A.alloc([128, 192], BF16) for _ in range(4)]
        qn = [A.alloc([128, T], BF16) for _ in range(2)]
        qr = [A.alloc([128, T], BF16) for _ in range(2)]
        knT = [A.alloc([128, TK], BF16) for _ in range(2)]
        V = [A.alloc([128, NK, 128], BF16) for _ in range(2)]
        PT = [A.alloc([128, 512], BF16) for _ in range(4)]
        of_ = [A.alloc([128, 512], F32) for _ in range(2)]
        rec = [A.alloc([128, 512], F32) for _ in range(2)]
        gt = [A.alloc([128, 512], BF16) for _ in range(2)]
        RTs = [(A.alloc([128, 1, 64], F32), A.alloc([128, 1, 64], F32)) for _ in range(2)]
        for n4 in range(3):
            dma('pool', wuq[:, :, n4 * 512:(n4 + 1) * 512], D["w_uq"][l, :, n4 * 512:(n4 + 1) * 512].rearrange("(kc p) n -> p kc n", p=128), [], ['wuq'])
        for n4 in range(4):
            dma('pool', wukv[:, :, n4 * 512:(n4 + 1) * 512], D["w_ukv"][l, :, n4 * 512:(n4 + 1) * 512].rearrange("(kc p) n -> p kc n", p=128), [], ['wukv'])
        for i in range(2):
            P.add('dve', lambda e, i=i: e.memset(qr[i][:], 0.0), [], [('qrm', i)] + [('qr', i, c) for c in range(NCH)])
            dma('pool', qr[i][64:96, :], D["qmask"][:, :], [], [('qrm', i)])
        ptc = {'i': 0}
        qblocks = blocks_of(T)
        kblocks = blocks_of(TK)
        _stop('A0')
        for h in range(8):
            hi = h % 2
            def qA(c):
                bk_, bkk = fb4()
                for kc in range(4):
                    mm(bk_[:, 0:192], cqnT[:, kc, c * 128:(c + 1) * 128], wuq[:, kc, h * 192:(h + 1) * 192], kc == 0, kc == 3, ['wuq', ('cqnT', c)], [bkk])
                q_, qk_ = qb[c % 4], ('qb', c % 4)
                cp('act', q_[:, 0:128], bk_[:, 0:128], [bkk], [qk_])
                rope(bk_[:, 128:192].rearrange("p (n d) -> p n d", n=1), bkk, c, q_[:, 128:192].rearrange("p (n d) -> p n d", n=1), qk_, 1, c % 2)

            def qB(c):
                q_, qk_ = qb[c % 4], ('qb', c % 4)
                t_, tk = tbk()
                tr(t_[:, 0:128], q_[:, 0:128], [qk_], [tk])
                tr(t_[0:64, 128:256], q_[:, 128:192], [qk_], [tk])
                cp('act', qn[hi][:, c * 128:(c + 1) * 128], t_[:, 0:128], [tk], [('qn', hi, c)])
                cp('act', qr[hi][0:64, c * 128:(c + 1) * 128], t_[0:64, 128:256], [tk], [('qr', hi, c)])
            qA(0)
            for c in range(NCH):
                if c + 1 < NCH:
                    qA(c + 1)
                qB(c)
            _stop('A1')
            for bi, (b0, bw) in enumerate(kblocks):
                bk_, bkk = fb4()
                for kc in range(4):
                    mm(bk_[:, 0:bw], wukv[:, kc, h * 256:h * 256 + 128], ckvnT[:, kc, b0:b0 + bw], kc == 0, kc == 3,
                       ['wukv'] + [('ckvnT', j) for j in range(b0 // 128, (b0 + bw) // 128)], [bkk])
                evcp(knT[hi][:, b0:b0 + bw], bk_[:, 0:bw], [bkk], [('knT', hi, bi)])
            _stop('A2')
            for j in range(NK):
                bk_, bkk = fb4()
                for kc in range(4):
                    mm(bk_[:, 0:128], ckvnT[:, kc, j * 128:(j + 1) * 128], wukv[:, kc, h * 256 + 128:h * 256 + 256], kc == 0, kc == 3,
                       ['wukv', ('ckvnT', j)], [bkk])
                evcp(V[hi][:, j, :], bk_[:, 0:128], [bkk], [('V', hi, j // 4)])
            _stop('A3')
            for bi, (b0, bw) in enumerate(qblocks):
                qres = [('qn', hi, c) for c in range(b0 // 128, (b0 + bw) // 128)] + [('qr', hi, c) for c in range(b0 // 128, (b0 + bw) // 128)] + [('qrm', hi)]
                Ob, Ok = pbf[4], ('pb', 4)
                Sb, Sk = pbf[5], ('pb', 5)
                i2 = bi % 2
                dma('sp', gt[i2][:, 0:bw], D["zT"][h * 128:(h + 1) * 128, b0:b0 + bw], [('zT', h, bi)], [('gt', i2)])

                def s_issue(j):
                    bk_, bkk = fb4()
                    mm(bk_[:, 0:bw], knT[hi][:, j * 128:(j + 1) * 128], qn[hi][:, b0:b0 + bw], True, False, [('knT', hi, j // 4)] + qres, [bkk])
                    mm(bk_[:, 0:bw], kropeT[:, j * 128:(j + 1) * 128], qr[hi][:, b0:b0 + bw], False, True, [('kropeT', j), 'kmaskrows'] + qres, [bkk])
                    return bk_, bkk
                LA = 2
                pend_s = [s_issue(j) for j in range(min(LA, NK))]
                for j in range(NK):
                    bk_, bkk = pend_s.pop(0)
                    ptc['i'] += 1
                    pi = ptc['i'] % 4
                    act(PT[pi][:, 0:bw], bk_[:, 0:bw], AF.Exp, [bkk], [('PT', pi)], scale=192 ** -0.5)
                    if j + LA < NK:
                        pend_s.append(s_issue(j + LA))
                    mm(Ob[:, 0:bw], V[hi][:, j, :], PT[pi][:, 0:bw], j == 0, j == NK - 1, [('V', hi, j // 4), ('PT', pi)], [Ok])
                    mm(Sb[:, 0:bw], ones[:, :], PT[pi][:, 0:bw], j == 0, j == NK - 1, ['ones', ('PT', pi)], [Sk])
                _stop('A4')
                recip(rec[i2][:, 0:bw], Sb[:, 0:bw], [Sk], [('rec', i2)])
                tt('dve', of_[i2][:, 0:bw], Ob[:, 0:bw], rec[i2][:, 0:bw], ALU.mult, [Ok, ('rec', i2)], [('of', i2)])
                s_, sk_ = stag_new()
                tt('pool', s_[:, 0:bw], of_[i2][:, 0:bw], gt[i2][:, 0:bw], ALU.mult, [('of', i2), ('gt', i2)], [sk_])
                dma('sp', D["gT"][0, h * 128:(h + 1) * 128, b0:b0 + bw], s_[:, 0:bw], [sk_], [('gT', 0, h, bi)])
        P.barrier()

        _stop('ATT')
        A.off = base_mark
        gsb = A.alloc([128, 1024], F32)
        bsT = A.alloc([128, 1024], F32)
        wsT = A.alloc([128, 1024], BF16)
        vs = [A.alloc([128, 1024], BF16) for _ in range(2)]
        vns = [A.alloc([128, 1024], BF16) for _ in range(2)]
        uT = [A.alloc([128, 8, 128], BF16) for _ in range(2)]
        gpT = [A.alloc([128, 8, 128], BF16) for _ in range(2)]
        t1s = [A.alloc([128, 1024], F32) for _ in range(2)]
        t2s = [A.alloc([128, 1024], F32) for _ in range(2)]
        go = [A.alloc([128, 8, 128], BF16) for _ in range(2)]
        dma('sp', gsb[:], D["g_sgu"][l:l + 1, :].partition_broadcast(128), [], ['gsb'])
        dma('sp', bsT[:], D["bsT"][l:l + 1, :].partition_broadcast(128), [], ['bsT'])
        dma('pool', wsT[:], D["wsT"][l], [], ['wsT'])
        zu = D["zT"][1024:2048, :].rearrange("(g p) t -> p g t", p=128)
        zg = D["zT"][2048:3072, :].rearrange("(g p) t -> p g t", p=128)
        gsg = D["gT"][1].rearrange("(g p) t -> p g t", p=128)
        def sgu_load(c):
            i = c % 2
            bi = (c * 128) // 512
            dma('sp', vs[i][:], D["ztm"][c * 128:(c + 1) * 128, 0:1024], [('ztm', c, 0), ('ztm', c, 1)], [('vs', i)])
            dma('sp', uT[i][:], zu[:, :, c * 128:(c + 1) * 128], [('zT', 8 + g, bi) for g in range(8)], [('uT', i)])
            dma('sp', gpT[i][:], zg[:, :, c * 128:(c + 1) * 128], [('zT', 16 + g, bi) for g in range(8)], [('gpT', i)])
        sgu_load(0)
        for c in range(NCH):
            i = c % 2
            if c + 1 < NCH:
                sgu_load(c + 1)
            vn, t1, t2 = vns[i], t1s[i], t2s[i]
            rs, sk = norm_stat(vs[i][:], ('vs', i), 1024, t1[:], ('t1', i))
            stt(vn[:], vs[i][:], rs, gsb[:], ALU.mult, ALU.mult, [('vs', i), sk, 'gsb'], [('vn', i)])
            for g4 in range(2):
                bk_, bkk = fb()
                for g in range(4):
                    gg = g4 * 4 + g
                    mm(bk_[:, g * 128:(g + 1) * 128], vn[:, gg * 128:(gg + 1) * 128], wsT[:, gg * 128:(gg + 1) * 128], True, True, [('vn', i), 'wsT'], [bkk])
                tt('dve', t2[:, g4 * 512:(g4 + 1) * 512], bk_[:, :], bsT[:, g4 * 512:(g4 + 1) * 512], ALU.add, [bkk, 'bsT'], [('t2', i)])
            tt('pool', t1[:], t2[:], uT[i][:].rearrange("p g t -> p (g t)"), ALU.mult, [('t2', i), ('uT', i)], [('t1', i)])
            tt('pool', go[i][:].rearrange("p g t -> p (g t)"), t1[:], gpT[i][:].rearrange("p g t -> p (g t)"), ALU.mult, [('t1', i), ('gpT', i)], [('go', i)])
            dma('sp', gsg[:, :, c * 128:(c + 1) * 128], go[i][:], [('go', i)], [('gT', 1, c)])
        P.barrier()

        _stop('SGU')
        A.off = base_mark
        grb = A.alloc([128, 1024], F32)
        lg = A.alloc([128, 16], F32)
        lt = A.alloc([128, 16], F32)
        Dt = A.alloc([128, 8, 128], F32)
        Dt2 = A.alloc([128, 8, 128], F32)
        xif = A.alloc([128, 8, 128], F32)
        xib = A.alloc([128, 8, 128], F32)
        zf = A.alloc([128, 8], F32)
        zb = A.alloc([128, 8], F32)
        gcv = A.alloc([128, 16], F32)
        zin = [A.alloc([128, 4096], BF16) for _ in range(3)]
        Rs = A.alloc([128, 8, 128], F32)
        Rin = A.alloc([128, 8, 128], F32)
        Rbf = [A.alloc([128, 8, 128], BF16) for _ in range(2)]
        Bn = [A.alloc([128, 8, 128], BF16) for _ in range(2)]
        D2 = lambda shape, dt: [A.alloc(shape, dt) for _ in range(2)]
        kzs, qTs, kTs, qxfs, qxbs, aTs, gros = (D2([128, 8, 128], BF16) for _ in range(7))
        of3s, cens, sq3s = (D2([128, 8, 128], F32) for _ in range(3))
        grT = [A.alloc([128, 8, 128], BF16) for _ in range(2)]
        dma('sp', grb[:], D["g_ret"][l:l + 1, :].partition_broadcast(128), [], ['grb'])
        dma('sp', lt[:], D["ret_decay"][l:l + 1, :].partition_broadcast(128), [], ['lt'])
        act(lt[:], lt[:], AF.Exp, ['lt'], ['lt'], scale=-1.0)
        act(lt[:], lt[:], AF.Ln, ['lt'], ['lt'], bias=1.0)
        ts(lg[:], lt[:], -1.0, None, ALU.mult, ALU.bypass, ['lt'], ['lg'])
        for h in range(8):
            act(Dt[:, h, :], dexf[:], AF.Exp, ['dexf', 'lg'], ['Dt'], scale=lg[:, h:h + 1])
            act(Dt2[:, h, :], dexb[:], AF.Exp, ['dexb', 'lg'], ['Dt2'], scale=lg[:, 8 + h:9 + h])
            act(xif[:, h, :], xiexp[:, 0:128], AF.Exp, ['xiexp', 'lg'], ['xif'], scale=lg[:, h:h + 1])
            act(xib[:, h, :], xiexp[:, 128:256], AF.Exp, ['xiexp', 'lg'], ['xib'], scale=lg[:, 8 + h:9 + h])
        tt('dve', Dt[:], Dt[:], Dt2[:], ALU.add, ['Dt', 'Dt2'], ['Dt'])
        act(zf[:], lg[:, 0:8], AF.Exp, ['lg', 'zexp'], ['zf'], scale=zexp[:, 0:1])
        act(zb[:], lg[:, 8:16], AF.Exp, ['lg', 'zexp'], ['zb'], scale=zexp[:, 1:2])
        act(gcv[:], lg[:], AF.Exp, ['lg'], ['gcv'], scale=128.0)

        def kz_make(zi, zk, zeta, zetak, i):
            k3 = zi[:, 1024:2048].rearrange("p (h d) -> p h d", h=8)
            tt('dve', kzs[i][:], k3, zeta[:].unsqueeze(2).broadcast_to([128, 8, 128]), ALU.mult, [zk, zetak], [('kz', i)])

        def state_update(zi, zk, gc, i):
            tt('pool', Rs[:], Rin[:], gc.unsqueeze(2).broadcast_to([128, 8, 128]), ALU.mult, ['Rin', 'gcv'], ['Rs'])
            for h4 in range(2):
                bk_, bkk = fb()
                for hh in range(4):
                    h = h4 * 4 + hh
                    mm(bk_[:, hh * 128:(hh + 1) * 128], kzs[i][:, h, :], zi[:, 2048 + h * 128:2048 + (h + 1) * 128], True, True, [('kz', i), zk], [bkk])
                tt('dve', Rs[:, h4 * 4:h4 * 4 + 4, :], Rs[:, h4 * 4:h4 * 4 + 4, :], bk_[:, :].rearrange("p (h e) -> p h e", h=4), ALU.add, ['Rs', bkk], ['Rs'])

        dma('sp', Rs[:], D["r0"][l, 1].rearrange("h d e -> d h e"), [], ['Rs'])
        def s1_load(n):
            dma('sp', zin[n % 2][:, 1024:3072], D["ztm"][n * 128:(n + 1) * 128, 2048:4096], [('ztm', n, 2 + q) for q in range(2, 6)], [('zin', n % 2)])
        s1_load(NCH - 1)
        for n in range(NCH - 1, -1, -1):
            i = n % 2
            if n - 1 >= 0:
                s1_load(n - 1)
            kz_make(zin[i], ('zin', i), zb, 'zb', i)
            ts(Rin[:], Rs[:], carry[:, NCH + n:NCH + n + 1], None, ALU.mult, ALU.bypass, ['Rs', 'carry'], ['Rin'])
            cp('pool', Rbf[i][:], Rin[:], ['Rin'], [('Rbf', i)])
            dma('sp', D["bst"][n].rearrange("p (h e) -> p h e", h=8), Rbf[i][:], [('Rbf', i)], [('bst', n)])
            state_update(zin[i], ('zin', i), gcv[:, 8:16], i)
            if n % 2 == 0:
                dma('sp', D["o_ret"][l, 1, n // 2].rearrange("h d e -> d h e"), Rs[:], ['Rs'], [])
        dma('sp', Rs[:], D["r0"][l, 0].rearrange("h d e -> d h e"), [], ['Rs'])
        grg = D["gT"][2].rearrange("(g p) t -> p g t", p=128)
        def s2_load(n):
            dma('sp', zin[n % 3][:], D["ztm"][n * 128:(n + 1) * 128, 1024:5120], [('ztm', n, 2 + q) for q in range(8)], [('zin', n % 3)])
            dma('sp', Bn[n % 2][:], D["bst"][n].rearrange("p (h e) -> p h e", h=8), [('bst', n)], [('Bn', n % 2)])
        s2_load(0)

        def s2_front(n):
            i = n % 2
            zi, zk = zin[n % 3], ('zin', n % 3)
            qT, kT, qxf, qxb, aT, gro, of3, cen, sq3 = qTs[i], kTs[i], qxfs[i], qxbs[i], aTs[i], gros[i], of3s[i], cens[i], sq3s[i]
            if n + 1 < NCH:
                s2_load(n + 1)
            for which, dstT, dk in ((0, qT, ('qT', i)), (1, kT, ('kT', i))):
                for h4 in range(2):
                    t_, tk = tbk()
                    for hh in range(4):
                        h = h4 * 4 + hh
                        tr(t_[:, hh * 128:(hh + 1) * 128], zi[:, which * 1024 + h * 128:which * 1024 + (h + 1) * 128], [zk], [tk])
                    evcp(dstT[:, h4 * 4:h4 * 4 + 4, :], t_[:, 0:512].rearrange("p (h t) -> p h t", h=4), [tk], [dk])
            tt('dve', qxf[:], qT[:], xif[:], ALU.mult, [('qT', i), 'xif'], [('qxf', i)])
            tt('pool', qxb[:], qT[:], xib[:], ALU.mult, [('qT', i), 'xib'], [('qxb', i)])
            kz_make(zi, zk, zf, 'zf', i)
            for h4 in range(2):
                bk_, bkk = fb()
                for hh in range(4):
                    h = h4 * 4 + hh
                    mm(bk_[:, hh * 128:(hh + 1) * 128], kT[:, h, :], qT[:, h, :], True, True, [('kT', i), ('qT', i)], [bkk])
                tt('dve', aT[:, h4 * 4:h4 * 4 + 4, :], bk_[:, :].rearrange("p (h t) -> p h t", h=4), Dt[:, h4 * 4:h4 * 4 + 4, :], ALU.mult, [bkk, 'Dt'], [('aT', i)])
            ts(Rin[:], Rs[:], carry[:, n:n + 1], None, ALU.mult, ALU.bypass, ['Rs', 'carry'], ['Rin'])
            cp('pool', Rbf[i][:], Rin[:], ['Rin'], [('Rbf', i)])
            for h4 in range(2):
                bk_, bkk = fb()
                for hh in range(4):
                    h = h4 * 4 + hh
                    o_ = bk_[:, hh * 128:(hh + 1) * 128]
                    mm(o_, aT[:, h, :], zi[:, 2048 + h * 128:2048 + (h + 1) * 128], True, False, [('aT', i), zk], [bkk])
                    mm(o_, qxb[:, h, :], Bn[i][:, h, :], False, False, [('qxb', i), ('Bn', i)], [bkk])
                    mm(o_, qxf[:, h, :], Rbf[i][:, h, :], False, True, [('qxf', i), ('Rbf', i)], [bkk])
                cp('act', of3[:, h4 * 4:h4 * 4 + 4, :], bk_[:, :].rearrange("p (h e) -> p h e", h=4), [bkk], [('of3', i)])
            state_update(zi, zk, gcv[:, 0:8], i)
            if n % 2 == 1 or NCH == 1:
                dma('sp', D["o_ret"][l, 0, n // 2].rearrange("h d e -> d h e"), Rs[:], ['Rs'], [])

        def s2_tail(n):
            i = n % 2
            zi, zk = zin[n % 3], ('zin', n % 3)
            gro, of3, cen, sq3 = gros[i], of3s[i], cens[i], sq3s[i]
            st, sk = st_new()
            P.add('dve', lambda e, st=st, of3=of3: e.tensor_reduce(out=st[:, 0:8], in_=of3[:], axis=AX.X, op=ALU.add), [('of3', i)], [sk])
            stt(cen[:], st[:, 0:8].unsqueeze(2).broadcast_to([128, 8, 128]), -1.0 / 128, of3[:], ALU.mult, ALU.add, [sk, ('of3', i)], [('cen', i)])
            act(sq3[:], cen[:], AF.Square, [('cen', i)], [('sq3', i)])
            st2, sk2 = st_new()
            P.add('dve', lambda e, st2=st2, sq3=sq3: e.tensor_reduce(out=st2[:, 0:8], in_=sq3[:], axis=AX.X, op=ALU.add), [('sq3', i)], [sk2])
            act(st2[:, 0:8], st2[:, 0:8], AF.Sqrt, [sk2], [sk2], scale=1.0 / 128, bias=EPS)
            recip(st2[:, 0:8], st2[:, 0:8], [sk2], [sk2])
            tt('dve', cen[:], cen[:], st2[:, 0:8].unsqueeze(2).broadcast_to([128, 8, 128]), ALU.mult, [('cen', i), sk2], [('cen', i)])
            tt('pool', sq3[:].rearrange("p h e -> p (h e)"), cen[:].rearrange("p h e -> p (h e)"), grb[:], ALU.mult, [('cen', i), 'grb'], [('sq3', i)])
            tt('pool', gro[:].rearrange("p h e -> p (h e)"), sq3[:].rearrange("p h e -> p (h e)"), zi[:, 3072:4096], ALU.mult, [('sq3', i), zk], [('gro', i)])
            for h4 in range(2):
                t_, tk = tbk()
                for hh in range(4):
                    h = h4 * 4 + hh
                    tr(t_[:, hh * 128:(hh + 1) * 128], gro[:, h, :], [('gro', i)], [tk])
                evcp(grT[i][:, h4 * 4:h4 * 4 + 4, :], t_[:, 0:512].rearrange("p (h t) -> p h t", h=4), [tk], [('grT', i)])
            dma('sp', grg[:, :, n * 128:(n + 1) * 128], grT[i][:], [('grT', i)], [('gT', 2, n)])
        s2_front(0)
        for n in range(NCH):
            if n + 1 < NCH:
                s2_front(n + 1)
            s2_tail(n)
        P.barrier()

        _stop('RET')
        A.off = base_mark
        wbr = [A.alloc([128, 8, DM], BF16) for _ in range(3)]
        gB = [[A.alloc([128, 8, 512], BF16) for _ in range(3)] for _ in range(2)]
        mB = [A.alloc([128, 3, 512], BF16) for _ in range(2)]
        tqs = [[A.alloc([128, 512], F32) for _ in range(3)] for _ in range(2)]
        for n4 in range(4):
            for j in range(3):
                dma('pool', wbr[j][:, :, n4 * 512:(n4 + 1) * 512], D["w_br"][l, j, :, n4 * 512:(n4 + 1) * 512].rearrange("(kc p) n -> p kc n", p=128), [], [('wbr', j, n4)])
        zm = D["zT"][3072:9216, :].rearrange("(j fc p) t -> fc p j t", j=3, p=128)
        def gB_load(bi):
            b0, bw = qblocks[bi]
            i = bi % 2
            gres = [[('gT', 0, h, bi) for h in range(8)], [('gT', 1, c) for c in range(b0 // 128, (b0 + bw) // 128)], [('gT', 2, c) for c in range(b0 // 128, (b0 + bw) // 128)]]
            for j in range(3):
                dma('sp', gB[i][j][:, :, 0:bw], D["gT"][j].rearrange("(kc p) t -> p kc t", p=128)[:, :, b0:b0 + bw], gres[j], [('gB', i, j)])

        def mB_load(it):
            bi, fc = divmod(it, 16)
            b0, bw = qblocks[bi]
            dma('sp', mB[fc % 2][:, :, 0:bw], zm[fc][:, :, b0:b0 + bw], [('zT', 24 + j * 16 + fc, bi) for j in range(3)], [('mB', fc % 2)])
        gB_load(0)
        mB_load(0)
        for bi, (b0, bw) in enumerate(qblocks):
            i = bi % 2
            if bi + 1 < len(qblocks):
                gB_load(bi + 1)
            for fc in range(16):
                m_, mk = mB[fc % 2], ('mB', fc % 2)
                tq, f2 = tqs[fc % 2], fc % 2
                if bi * 16 + fc + 1 < 16 * len(qblocks):
                    mB_load(bi * 16 + fc + 1)
                for j in range(3):
                    bk_, bkk = fb()
                    for kc in range(8):
                        mm(bk_[:, 0:bw], wbr[j][:, kc, fc * 128:(fc + 1) * 128], gB[i][j][:, kc, 0:bw], kc == 0, kc == 7, [('wbr', j, fc // 4), ('gB', i, j)], [bkk])
                    tt('dve', tq[j][:, 0:bw], bk_[:, 0:bw], m_[:, j, 0:bw], ALU.mult, [bkk, mk], [('tq', f2, j)])
                tt('pool', tq[0][:, 0:bw], tq[0][:, 0:bw], tq[1][:, 0:bw], ALU.add, [('tq', f2, 0), ('tq', f2, 1)], [('tq', f2, 0)])
                s_, sk_ = stag_new()
                tt('pool', s_[:, 0:bw], tq[0][:, 0:bw], tq[2][:, 0:bw], ALU.add, [('tq', f2, 0), ('tq', f2, 2)], [sk_])
                dma('sp', D["ysT"][fc * 128:(fc + 1) * 128, b0:b0 + bw], s_[:, 0:bw], [sk_], [('ysT', fc, bi)])
        P.barrier()

        _stop('OUT')
        A.off = base_mark
        wout = A.alloc([128, 16, DM], BF16)
        wt = [A.alloc([128, 16, 512], BF16) for _ in range(2)]
        Gb = A.alloc([128, DM], F32)
        ysB = [A.alloc([128, 16, 512], BF16) for _ in range(2)]
        xc = [A.alloc([128, DM], F32) for _ in range(2)]
        yos = [A.alloc([128, DM], F32) for _ in range(2)]
        yo = yos[0]
        junks = [A.alloc([128, 512], BF16) for _ in range(2)]
        mod_third(2, Gb, 'Gb')
        for n4 in range(4):
            dma('pool', wout[:, :, n4 * 512:(n4 + 1) * 512], D["w_out"][l, :, n4 * 512:(n4 + 1) * 512].rearrange("(kc p) n -> p kc n", p=128), [], [('wout', n4)])
        dma('sp', yo[:], D["g_post"][l:l + 1, :].partition_broadcast(128), [], [('yo', 0)])
        tt('dve', Gb[:], Gb[:], yo[:], ALU.mult, ['Gb', ('yo', 0)], ['Gb'])
        def ys_load(bi):
            b0, bw = qblocks[bi]
            dma('sp', ysB[bi % 2][:, :, 0:bw], D["ysT"].rearrange("(kc p) t -> p kc t", p=128)[:, :, b0:b0 + bw], [('ysT', fc, bi) for fc in range(16)], [('ysB', bi % 2)])

        def x_load(c):
            dma('sp', xc[c % 2][:], xsrc[c * 128:(c + 1) * 128, :], [(xkey, c)], [('xc', c % 2)])
        ys_load(0)
        x_load(0)
        for bi, (b0, bw) in enumerate(qblocks):
            i = bi % 2
            if bi + 1 < len(qblocks):
                ys_load(bi + 1)
            for cc in range(bw // 128):
                c = b0 // 128 + cc
                x_, xk = xc[c % 2], ('xc', c % 2)
                yo, yok = yos[c % 2], ('yo', c % 2)
                if c + 1 < NCH:
                    x_load(c + 1)
                st, sk = st_new()
                bks = []
                for nt in range(4):
                    bk_, bkk = fb()
                    bks.append((bk_, bkk))
                    for kc in range(16):
                        mm(bk_[:, :], ysB[i][:, kc, cc * 128:(cc + 1) * 128], wout[:, kc, nt * 512:(nt + 1) * 512], kc == 0, kc == 15, [('ysB', i), ('wout', nt)], [bkk])
                    act(junks[nt % 2][:], bk_[:, :], AF.Square, [bkk], [('junk', nt % 2), sk], accum=st[:, nt:nt + 1])
                    tt('dve', yo[:, nt * 512:(nt + 1) * 512], bk_[:, :], Gb[:, nt * 512:(nt + 1) * 512], ALU.mult, [bkk, 'Gb'], [yok])
                P.add('dve', lambda e, st=st: e.tensor_reduce(out=st[:, 4:5], in_=st[:, 0:4], axis=AX.X, op=ALU.add), [sk], [sk])
                act(st[:, 5:6], st[:, 4:5], AF.Sqrt, [sk], [sk], scale=1.0 / DM, bias=EPS)
                recip(st[:, 6:7], st[:, 5:6], [sk], [sk])
                stt(yo[:], yo[:], st[:, 6:7], x_[:], ALU.mult, ALU.add, [yok, sk, xk], [yok])
                dma('sp', D["y"][c * 128:(c + 1) * 128, :], yo[:], [yok, (xkey, c)], [('y', c)])
        P.barrier()
    try:
        for l in range(NL):
            layer(l)
    except StopBuild:
        P.barrier()
    P.emit(nc)
    return nc


def _consts(NCH, kind, nseq=0):
    T = NCH * 128
    TK = T + 256
    c = {}
    if kind == 'sample':
        pos = np.arange(T)
        row, col = (pos // 64).astype(np.float32), (pos % 64).astype(np.float32)
        inv = (10000.0 ** (-np.arange(16, dtype=np.float32) / 16)).astype(np.float32)
        ar, ac = row[:, None] * inv, col[:, None] * inv
        cr, sr, cc, sc = np.cos(ar), np.sin(ar), np.cos(ac), np.sin(ac)
        c["ropeC"] = np.concatenate([cr, cr, cc, cc], 1).astype(np.float32)
        c["ropeS"] = np.concatenate([-sr, sr, -sc, sc], 1).astype(np.float32)
        qm = np.zeros((32, T), np.float32); qm[0] = 1
        km = np.zeros((32, TK), np.float32)
        cf = np.ones(NCH, np.float32); cb = np.ones(NCH, np.float32)
    else:
        c["ropeC"] = np.ones((T, 64), np.float32)
        c["ropeS"] = np.zeros((T, 64), np.float32)
        qm = np.zeros((32, T), np.float32)
        km = np.full((32, TK), -BIG, np.float32)
        km[16:] = 0
        for s in range(T // 256):
            qm[s, s * 256:(s + 1) * 256] = 1
            km[s, 256 + s * 256:256 + (s + 1) * 256] = 0
        cf = np.array([0.0 if n % 2 == 0 else 1.0 for n in range(NCH)], np.float32)
        cb = np.array([0.0 if n % 2 == 1 else 1.0 for n in range(NCH)], np.float32)
    c["qmask"], c["kmask"] = qm, km
    c["carry"] = np.tile(np.concatenate([cf, cb])[None, :], (128, 1)).astype(np.float32)
    j = np.arange(128)[:, None].astype(np.float32); i = np.arange(128)[None, :].astype(np.float32)
    c["dexf"] = np.where(i >= j, i - j, 1e6).astype(np.float32)
    c["dexb"] = np.where(j > i, j - i, 1e6).astype(np.float32)
    c["xiexp"] = np.tile(np.concatenate([np.arange(128) + 1.0, 128.0 - np.arange(128)])[None, :], (128, 1)).astype(np.float32)
    c["zexp"] = np.stack([127.0 - np.arange(128), np.arange(128) * 1.0], 1).astype(np.float32)
    c["ident"] = np.eye(128, dtype=np.float32)
    return c


def _shared(inp, NL):
    f = lambda a: np.ascontiguousarray(a, dtype=np.float32)
    return {
        "w_mod": f(inp["w_mod"][:NL]), "b_mod": f(inp["b_mod"][:NL]), "g_pre": f(inp["g_pre"][:NL]), "g_post": f(inp["g_post"][:NL]),
        "w_in": f(inp["w_in"][:NL]), "g_q": f(inp["g_q"][:NL]), "g_kv": f(inp["g_kv"][:NL]), "w_uq": f(inp["w_uq"][:NL]),
        "w_ukv": f(inp["w_ukv"][:NL]), "g_sgu": f(inp["g_sgu"][:NL]),
        "wsT": f(np.transpose(inp["w_sgu"][:NL], (0, 3, 1, 2)).reshape(NL, 128, 1024)),
        "bsT": f(np.transpose(inp["b_sgu"][:NL], (0, 2, 1)).reshape(NL, 1024)),
        "ret_decay": f(inp["ret_decay"][:NL].reshape(NL, 16)), "g_ret": f(inp["g_ret"][:NL]),
        "w_br": f(np.stack([inp["w_br_mla"][:NL], inp["w_br_sgu"][:NL], inp["w_br_ret"][:NL]], 1)), "w_out": f(inp["w_out"][:NL]),
    }


def _core(x, cond, cckv, ckr, r0, NCH, kind):
    d = {"xin": np.ascontiguousarray(x, np.float32), "cond": np.ascontiguousarray(cond.reshape(16, 128).T, np.float32),
         "cckv": np.ascontiguousarray(cckv, np.float32), "ckr": np.ascontiguousarray(ckr, np.float32), "r0": np.ascontiguousarray(r0, np.float32)}
    d.update(_consts(NCH, kind))
    return d


_NC_CACHE = {}


def run_cores(cores, shared, NL, NCH):
    key = (NL, NCH)
    if key not in _NC_CACHE:
        _NC_CACHE[key] = build(NL, NCH)
    nc = _NC_CACHE[key]
    in_maps = [dict(c, **shared) for c in cores]
    res = run_bass_kernel_spmd(nc, in_maps, core_ids=list(range(len(cores))))
    return res.results


def kernel(**inp):
    NL, NCH = 4, 16
    shared = _shared(inp, NL)
    z = lambda *s: np.zeros(s, np.float32)
    act_ = []
    for b in range(2):
        act_.append(_core(inp["x_sample"][b], inp["c"][b], inp["cache_ckv"][b], inp["cache_krope"][b], inp["state_ret"][b], NCH, 'sample'))
    for p in range(2):
        act_.append(_core(inp["x_prompt"][p * 8:(p + 1) * 8].reshape(2048, 2048), inp["c_ctx"], z(NL, 256, 512), z(NL, 256, 64), z(NL, 2, 8, 128, 128), NCH, 'prompt'))
    idle = _core(z(2048, 2048), z(2048), z(NL, 256, 512), z(NL, 256, 64), z(NL, 2, 8, 128, 128), NCH, 'sample')
    slot = [0, 4, 1, 5]
    zshared = {k: np.zeros_like(v) for k, v in shared.items()}
    cores = [dict(idle, **zshared)] * 8
    for a, sl in zip(act_, slot):
        cores[sl] = dict(a, **shared)
    res_all = run_cores(cores, {}, NL, NCH)
    res = [res_all[sl] for sl in slot]
    y_sample = np.stack([res[0]["y"], res[1]["y"]], 0).astype(np.float32)
    y_prompt = np.concatenate([res[2]["y"].reshape(8, 256, 2048), res[3]["y"].reshape(8, 256, 2048)], 0).astype(np.float32)
    new_ckv = np.concatenate([np.transpose(res[2 + p]["o_ckv"].reshape(NL, 8, 256, 512), (1, 0, 2, 3)) for p in range(2)], 0).astype(np.float32)
    new_kr = np.concatenate([np.transpose(res[2 + p]["o_kr"].reshape(NL, 8, 256, 64), (1, 0, 2, 3)) for p in range(2)], 0).astype(np.float32)
    new_ret = np.concatenate([np.transpose(res[2 + p]["o_ret"], (2, 0, 1, 3, 4, 5)) for p in range(2)], 0).astype(np.float32)
    return (y_prompt, y_sample, new_ckv, new_kr, new_ret)
```

```python
import contextlib
import os
import numpy as np
import concourse.bass as bass
import concourse.mybir as mybir
from concourse.bass_utils import run_bass_kernel_spmd

F32, BF16 = mybir.dt.float32, mybir.dt.bfloat16
AF = mybir.ActivationFunctionType
ALU = mybir.AluOpType
AX = mybir.AxisListType
EPS = 1e-6
DM, INC = 2048, 15424
C_CQ, C_CKV, C_KR, C_GPM, C_U, C_VS, C_GPS, C_RQ, C_GPR, C_MRG = 0, 512, 1024, 1088, 2112, 3136, 4160, 5184, 8256, 9280
BIG = 1.0e5
ENG = ['pe', 'act', 'dve', 'pool', 'sp']
KSEM = 6


class StopBuild(Exception):
    pass


def _stop(tag):
    if os.environ.get('K_STOP') == tag:
        raise StopBuild()


class Op:
    __slots__ = ('eng', 'fn', 'dma', 'sig', 'pos', 'deps', 'dsem', 'dval', 'cnt')


class Prog:
    def __init__(s):
        s.q = {e: [] for e in ENG}
        s.lastw, s.rd = {}, {}
        s.pend = {e: set() for e in ENG}
        s.dcount = {}
        s.rr = {e: 0 for e in ENG}
        s.dmas = []

    def add(s, eng, fn, r=(), w=(), dma=False):
        op = Op()
        op.eng, op.fn, op.dma, op.sig, op.pos = eng, fn, dma, False, len(s.q[eng])
        x_ = [('x',) + k for k in r if isinstance(k, tuple) and k[0] in ('pb', 'pt')]
        if x_:
            w = list(w) + x_
        deps = s.pend[eng]
        s.pend[eng] = set()
        for k in r:
            x = s.lastw.get(k)
            if x is not None:
                deps.add(x)
        for k in w:
            x = s.lastw.get(k)
            if x is not None:
                deps.add(x)
            d = s.rd.get(k)
            if d:
                for y in d.values():
                    if isinstance(y, list):
                        deps.update(y)
                    else:
                        deps.add(y)
        for k in r:
            d = s.rd.setdefault(k, {})
            if dma:
                d.setdefault('dma', []).append(op)
            else:
                d[eng] = op
        for k in w:
            s.lastw[k] = op
            s.rd[k] = {}
        deps.discard(op)
        op.deps = deps
        if dma:
            sem = (eng, s.rr[eng] % KSEM)
            s.rr[eng] += 1
            s.dcount[sem] = s.dcount.get(sem, 0) + 1
            op.dsem, op.dval = sem, 16 * s.dcount[sem]
            s.dmas.append(op)
        s.q[eng].append(op)
        return op

    def barrier(s):
        last = set(s.dmas)
        s.dmas = []
        for e in ENG:
            for o in reversed(s.q[e]):
                if not o.dma and o.fn is not None:
                    last.add(o)
                    break
        for e in ENG:
            s.pend[e] |= last

    def emit(s, nc):
        for e in ENG:
            if s.pend[e]:
                s.add(e, None)
        for e in ENG:
            for op in s.q[e]:
                for d in op.deps:
                    if not d.dma:
                        if d.eng == 'pe' and op.eng == 'pe':
                            continue
                        d.sig = True
        for e in ENG:
            c = 0
            for op in s.q[e]:
                if op.sig:
                    c += 1
                op.cnt = c
        with contextlib.ExitStack() as es:
            csem = {e: es.enter_context(nc.semaphore("c_" + e)) for e in ENG if e != 'sp'}
            dsem = {k: es.enter_context(nc.semaphore("d_%s%d" % k)) for k in s.dcount}
            block = es.enter_context(nc.Block())

            def run(e, eng):
                seen = {}

                def wait(sem, key, val):
                    if val > 0 and seen.get(key, 0) < val:
                        eng.wait_ge(sem, val)
                        seen[key] = val
                for op in s.q[e]:
                    for d in op.deps:
                        if d.dma:
                            wait(dsem[d.dsem], d.dsem, d.dval)
                        elif not (d.eng == 'pe' and e == 'pe'):
                            wait(csem[d.eng], d.eng, d.cnt)
                    if op.fn is None:
                        continue
                    if op.dma:
                        wait(dsem[op.dsem], op.dsem, op.dval - 16)
                        op.fn(eng).then_inc(dsem[op.dsem], 16)
                    else:
                        ins = op.fn(eng)
                        if op.sig:
                            ins.then_inc(csem[e], 1)
            block.tensor(lambda eng: run('pe', eng))
            block.scalar(lambda eng: run('act', eng))
            block.vector(lambda eng: run('dve', eng))
            block.gpsimd(lambda eng: run('pool', eng))
            block.sync(lambda eng: run('sp', eng))


class Arena:
    def __init__(s, nc, base=18432, top=229344):
        s.nc, s.off, s.top, s.n = nc, base, top, 0

    def alloc(s, shape, dt):
        nb = int(np.prod(shape[1:])) * (2 if dt == BF16 else 4)
        nb = (nb + 31) // 32 * 32
        h = s.nc.alloc_sbuf_tensor_at("t%d" % s.n, list(shape), dt, offset=s.off)
        s.n += 1
        s.off += nb
        assert s.off <= s.top, ("SBUF overflow", s.off)
        return h


def blocks_of(n, w=512):
    return [(b, min(w, n - b)) for b in range(0, n, w)]


def build(NL, NCH):
    T, NK = NCH * 128, NCH + 2
    TK = NK * 128
    nc = bass.Bass("TRN2", target_bir_lowering=False)
    P = Prog()
    D = {}

    def din(n, shape):
        D[n] = nc.dram_tensor(n, list(shape), F32, kind="ExternalInput").ap()

    def dout(n, shape):
        D[n] = nc.dram_tensor(n, list(shape), F32, kind="ExternalOutput").ap()

    def dscr(n, shape, dt):
        D[n] = nc.dram_tensor(n, list(shape), dt, kind="Internal").ap()
    din("xin", [T, DM]); din("cond", [128, 16]); din("cckv", [NL, 256, 512]); din("ckr", [NL, 256, 64])
    din("r0", [NL, 2, 8, 128, 128]); din("ropeC", [T, 64]); din("ropeS", [T, 64])
    din("qmask", [32, T]); din("kmask", [32, TK]); din("carry", [128, 2 * NCH])
    din("dexf", [128, 128]); din("dexb", [128, 128]); din("xiexp", [128, 256]); din("zexp", [128, 2])
    din("ident", [128, 128])
    din("w_mod", [NL, DM, 3 * DM]); din("b_mod", [NL, 3 * DM]); din("g_pre", [NL, DM]); din("g_post", [NL, DM])
    din("w_in", [NL, DM, INC]); din("g_q", [NL, 512]); din("g_kv", [NL, 512]); din("w_uq", [NL, 512, 1536])
    din("w_ukv", [NL, 512, 2048]); din("g_sgu", [NL, 1024]); din("wsT", [NL, 128, 1024]); din("bsT", [NL, 1024])
    din("ret_decay", [NL, 16]); din("g_ret", [NL, 1024]); din("w_br", [NL, 3, 1024, DM]); din("w_out", [NL, DM, DM])
    dout("y", [T, DM]); dout("o_ckv", [NL, T, 512]); dout("o_kr", [NL, T, 64])
    dout("o_ret", [NL, 2, max(NCH // 2, 1), 8, 128, 128])
    dscr("ztm", [T, 5120], BF16)
    dscr("zT", [9216, T], BF16)
    dscr("gT", [3, 1024, T], BF16)
    dscr("ysT", [DM, T], BF16)
    dscr("bst", [NCH, 128, 1024], BF16)
    dscr("modscr", [NL, 3, DM], F32)
    A = Arena(nc)
    pbf = [nc.alloc_psum_tensor("pb%d" % i, [128, 512], F32) for i in range(6)]
    pbt = [nc.alloc_psum_tensor("pt%d" % i, [128, 1024], BF16) for i in range(2)]
    cnt = {'fb': 0, 'tb': 0, 'st': 0, 'ev': 0}

    def fb():
        cnt['fb'] += 1
        i = cnt['fb'] % 6
        return pbf[i], ('pb', i)

    def fb4():
        cnt['fb'] += 1
        i = cnt['fb'] % 4
        return pbf[i], ('pb', i)

    def tbk():
        cnt['tb'] += 1
        i = cnt['tb'] % 2
        return pbt[i], ('pt', i)

    def mm(out, lhsT, rhs, start, stop, r, w):
        P.add('pe', lambda e: e.matmul(out, lhsT, rhs, start=start, stop=stop), r, w)

    def tr(out, in_, r, w):
        P.add('pe', lambda e: e.transpose(out, in_, ident[0:in_.shape[0], 0:in_.shape[0]]), list(r) + ['ident'], w)

    def act(out, in_, func, r, w, scale=1.0, bias=0.0, accum=None):
        if accum is None:
            P.add('act', lambda e: e.activation(out=out, in_=in_, func=func, scale=scale, bias=bias), r, w)
        else:
            P.add('act', lambda e: e.activation(out=out, in_=in_, func=func, scale=scale, bias=bias, accum_out=accum), r, w)

    def tt(eng, out, in0, in1, op, r, w):
        P.add(eng, lambda e: e.tensor_tensor(out=out, in0=in0, in1=in1, op=op), r, w)

    def stt(out, in0, scalar, in1, op0, op1, r, w, eng='dve'):
        P.add(eng, lambda e: e.scalar_tensor_tensor(out=out, in0=in0, scalar=scalar, in1=in1, op0=op0, op1=op1), r, w)

    def ts(out, in0, s1, s2, op0, op1, r, w, eng='dve'):
        if s2 is None:
            P.add(eng, lambda e: e.tensor_scalar(out=out, in0=in0, scalar1=s1, scalar2=None, op0=op0), r, w)
        else:
            P.add(eng, lambda e: e.tensor_scalar(out=out, in0=in0, scalar1=s1, scalar2=s2, op0=op0, op1=op1), r, w)

    def cp(eng, out, in_, r, w):
        if eng == 'act':
            P.add('act', lambda e: e.copy(out=out, in_=in_), r, w)
        else:
            P.add(eng, lambda e: e.tensor_copy(out=out, in_=in_), r, w)

    def evcp(out, in_, r, w):
        cnt['ev'] += 1
        cp('act' if cnt['ev'] % 2 else 'dve', out, in_, r, w)

    def recip(out, in_, r, w):
        P.add('dve', lambda e: e.reciprocal(out=out, in_=in_), r, w)

    def dma(eng, out, in_, r, w):
        P.add(eng, lambda e: e.dma_start(out=out, in_=in_), r, w, dma=True)

    def rstd_of(ssq, n, rk):
        t = ssq.tensor
        return None

    ident = A.alloc([128, 128], BF16)
    ones = A.alloc([128, 128], BF16)
    ropeC = A.alloc([128, NCH, 64], F32)
    ropeS = A.alloc([128, NCH, 64], F32)
    csrep = A.alloc([128, 16, 128], BF16)
    cs = A.alloc([128, 16], BF16)
    condt = A.alloc([128, 16], F32)
    carry = A.alloc([128, 2 * NCH], F32)
    kropeT = A.alloc([128, TK], BF16)
    stt_ring = [A.alloc([128, 8], F32) for _ in range(8)]
    dexf = A.alloc([128, 128], F32)
    dexb = A.alloc([128, 128], F32)
    xiexp = A.alloc([128, 256], F32)
    zexp = A.alloc([128, 2], F32)
    stag = [A.alloc([128, 512], BF16) for _ in range(4)]
    base_mark = A.off

    def st_new():
        cnt['st'] += 1
        i = cnt['st'] % 8
        return stt_ring[i], ('st', i)
    sg = {'i': 0}

    def stag_new():
        sg['i'] += 1
        i = sg['i'] % 4
        return stag[i], ('stag', i)

    dma('pool', ident[:], D["ident"][:, :], [], ['ident'])
    P.add('dve', lambda e: e.memset(ones[:], 1.0), [], ['ones'])
    dma('sp', ropeC[:], D["ropeC"].rearrange("(c p) d -> p c d", p=128), [], ['ropeC'])
    dma('sp', ropeS[:], D["ropeS"].rearrange("(c p) d -> p c d", p=128), [], ['ropeS'])
    dma('sp', condt[:], D["cond"][:, :], [], ['condt'])
    dma('sp', carry[:], D["carry"][:, :], [], ['carry'])
    dma('sp', dexf[:], D["dexf"][:, :], [], ['dexf'])
    dma('sp', dexb[:], D["dexb"][:, :], [], ['dexb'])
    dma('sp', xiexp[:], D["xiexp"][:, :], [], ['xiexp'])
    dma('sp', zexp[:], D["zexp"][:, :], [], ['zexp'])
    P.add('dve', lambda e: e.memset(kropeT[:], 0.0), [], ['kmaskrows'] + [('kropeT', j) for j in range(NK)])
    dma('pool', kropeT[64:96, :], D["kmask"][:, :], [], ['kmaskrows'])
    act(cs[:], condt[:], AF.Silu, ['condt'], ['cs'])
    cp('dve', csrep[:], cs[:].unsqueeze(2).broadcast_to([128, 16, 128]), ['cs'], ['csrep'])

    def norm_stat(bank_ap, bk, n, junk_ap, jk):
        st, sk = st_new()
        act(junk_ap, bank_ap, AF.Square, [bk], [jk, sk], accum=st[:, 0:1])
        act(st[:, 1:2], st[:, 0:1], AF.Sqrt, [sk], [sk], scale=1.0 / n, bias=EPS)
        recip(st[:, 2:3], st[:, 1:2], [sk], [sk])
        return st[:, 2:3], sk

    mt = {'i': 0}

    def mod_tile_load(lm, t, wm):
        i = t % 2
        dma('pool', wm[i][:], D["w_mod"][lm, :, t * 512:(t + 1) * 512].rearrange("(kc p) n -> p kc n", p=128), [], [('wm', i)])

    def mod_tile_compute(lm, t, wm, mst, bankfn):
        i = t % 2
        bk_, bkk = bankfn()
        for kc in range(16):
            mm(bk_[:, :], csrep[:, kc, :], wm[i][:, kc, :], kc == 0, kc == 15, [('wm', i), 'csrep'], [bkk])
        cp('act', mst[i][0:1, :], bk_[0:1, :], [bkk], [('mst', i)])
        j, n4 = divmod(t, 4)
        dma('sp', D["modscr"][lm, j:j + 1, n4 * 512:(n4 + 1) * 512], mst[i][0:1, :], [('mst', i)], [('modscr', lm, j, n4)])

    A.off = base_mark
    wm0 = [A.alloc([128, 16, 512], BF16) for _ in range(2)]
    mst0 = [A.alloc([128, 512], F32) for _ in range(2)]
    mod_tile_load(0, 0, wm0)
    for t in range(12):
        if t + 1 < 12:
            mod_tile_load(0, t + 1, wm0)
        mod_tile_compute(0, t, wm0, mst0, fb)
    P.barrier()

    def layer(l):
        xsrc = D["xin"] if l == 0 else D["y"]
        xkey = 'xin' if l == 0 else 'y'
        A.off = base_mark
        cqnT = A.alloc([128, 4, T], BF16)
        ckvnT = A.alloc([128, 4, TK], BF16)
        att_mark = A.off
        hT = A.alloc([128, 16, T], BF16)
        wt = [A.alloc([128, 16, 512], BF16) for _ in range(2)]
        Ab = A.alloc([128, DM], F32)
        Bb = A.alloc([128, DM], F32)
        xc = [A.alloc([128, DM], F32) for _ in range(2)]
        hbs = [A.alloc([128, DM], BF16) for _ in range(2)]
        gqb = A.alloc([128, 512], F32)
        gkvb = A.alloc([128, 512], F32)
        tf = [A.alloc([128, 512], F32) for _ in range(2)]
        tb = [A.alloc([128, 512], BF16) for _ in range(2)]
        r1 = A.alloc([128, 64], F32)
        wi = {'i': 0}

        def wt_new():
            wi['i'] += 1
            i = wi['i'] % 2
            return wt[i], ('wt', i)

        def mod_third(j, dst, dk):
            dma('sp', dst[:], D["b_mod"][l:l + 1, j * DM:(j + 1) * DM].partition_broadcast(128), [], [dk])
            for n4 in range(4):
                w_, wk = wt_new()
                c0 = j * DM + n4 * 512
                dma('pool', w_[:], D["w_mod"][l, :, c0:c0 + 512].rearrange("(kc p) n -> p kc n", p=128), [], [wk])
                bk_, bkk = fb()
                for kc in range(16):
                    mm(bk_[:, :], csrep[:, kc, :], w_[:, kc, :], kc == 0, kc == 15, [wk, 'csrep'], [bkk])
                tt('dve', dst[:, n4 * 512:(n4 + 1) * 512], bk_[:, :], dst[:, n4 * 512:(n4 + 1) * 512], ALU.add, [bkk, dk], [dk])

        dma('sp', gqb[:], D["g_q"][l:l + 1, :].partition_broadcast(128), [], ['gqb'])
        dma('sp', gkvb[:], D["g_kv"][l:l + 1, :].partition_broadcast(128), [], ['gkvb'])
        _stop('PRE')
        for j_, dst_, dk_ in ((0, Bb, 'Bb'), (1, Ab, 'Ab')):
            dma('sp', dst_[:], D["b_mod"][l:l + 1, j_ * DM:(j_ + 1) * DM].partition_broadcast(128), [], [dk_])
            dma('sp', xc[1][:], D["modscr"][l, j_:j_ + 1, :].partition_broadcast(128), [('modscr', l, j_, q) for q in range(4)], [('xc', 1)])
            tt('dve', dst_[:], dst_[:], xc[1][:], ALU.add, [dk_, ('xc', 1)], [dk_])
        dma('sp', xc[0][:], D["g_pre"][l:l + 1, :].partition_broadcast(128), [], [('xc', 0)])
        stt(Ab[:], Ab[:], 1.0, xc[0][:], ALU.add, ALU.mult, ['Ab', ('xc', 0)], ['Ab'])
        _stop('MOD')
        dma('sp', xc[0][:], xsrc[0:128, :], [(xkey, 0)], [('xc', 0)])
        for c in range(NCH):
            x_, xk = xc[c % 2], ('xc', c % 2)
            hb, hbk = hbs[c % 2], ('hb', c % 2)
            if c + 1 < NCH:
                dma('sp', xc[(c + 1) % 2][:], xsrc[(c + 1) * 128:(c + 2) * 128, :], [(xkey, c + 1)], [('xc', (c + 1) % 2)])
            rs, sk = norm_stat(x_[:], xk, DM, hb[:], hbk)
            stt(x_[:], x_[:], rs, Ab[:], ALU.mult, ALU.mult, [xk, sk, 'Ab'], [xk])
            tt('pool', hb[:], x_[:], Bb[:], ALU.add, [xk, 'Bb'], [hbk])
            for k4 in range(4):
                t_, tk = tbk()
                for j in range(4):
                    kc = k4 * 4 + j
                    tr(t_[:, j * 128:(j + 1) * 128], hb[:, kc * 128:(kc + 1) * 128], [hbk], [tk])
                evcp(hT[:, k4 * 4:k4 * 4 + 4, c * 128:(c + 1) * 128], t_[:, 0:512].rearrange("p (j t) -> p j t", j=4), [tk], [('hT', c)])
        _stop('H')
        for j in range(2):
            b_, bk_ = tb[j], ('tb', j)
            dma('pool', b_[:], D["cckv"][l, j * 128:(j + 1) * 128, :], [], [bk_])
            t_, tk = tbk()
            for q4 in range(4):
                tr(t_[:, q4 * 128:(q4 + 1) * 128], b_[:, q4 * 128:(q4 + 1) * 128], [bk_], [tk])
            evcp(ckvnT[:, :, j * 128:(j + 1) * 128], t_[:, 0:512].rearrange("p (j t) -> p j t", j=4), [tk], [('ckvnT', j)])
            s_, sk_ = stag_new()
            dma('pool', s_[:, 0:64], D["ckr"][l, j * 128:(j + 1) * 128, :], [], [sk_])
            t_, tk = tbk()
            tr(t_[0:64, 0:128], s_[:, 0:64], [sk_], [tk])
            cp('act', kropeT[0:64, j * 128:(j + 1) * 128], t_[0:64, 0:128], [tk], [('kropeT', j)])

        _stop('CTX')
        def tm_pass(c0, ncols, evac):
            w_, wk = wt_new()
            dma('pool', w_[:, :, 0:ncols], D["w_in"][l, :, c0:c0 + ncols].rearrange("(kc p) n -> p kc n", p=128), [], [wk])
            for c in range(NCH):
                bk_, bkk = fb()
                for kc in range(16):
                    mm(bk_[:, 0:ncols], hT[:, kc, c * 128:(c + 1) * 128], w_[:, kc, 0:ncols], kc == 0, kc == 15, [wk, ('hT', c)], [bkk])
                evac(c, bk_, bkk)

        def normed(c, bk_, bkk, gb, gk, outT, okey, coff, fp32_out):
            i = c % 2
            rs, sk = norm_stat(bk_[:, :], bkk, 512, tf[i][:], ('tf', i))
            if fp32_out is not None:
                stt(tf[i][:], bk_[:, :], rs, gb[:], ALU.mult, ALU.mult, [bkk, sk, gk], [('tf', i)])
                dma('sp', fp32_out[c * 128:(c + 1) * 128, :], tf[i][:], [('tf', i)], [])
                cp('pool', tb[i][:], tf[i][:], [('tf', i)], [('tb', i)])
            else:
                stt(tb[i][:], bk_[:, :], rs, gb[:], ALU.mult, ALU.mult, [bkk, sk, gk], [('tb', i)])
            t_, tk = tbk()
            for q4 in range(4):
                tr(t_[:, q4 * 128:(q4 + 1) * 128], tb[i][:, q4 * 128:(q4 + 1) * 128], [('tb', i)], [tk])
            evcp(outT[:, :, coff + c * 128:coff + (c + 1) * 128], t_[:, 0:512].rearrange("p (j t) -> p j t", j=4), [tk], [(okey, coff // 128 + c)])

        def rope(src, sk, c, dst, dk, nh, slot=0):
            Cv = ropeC[:, c, :].unsqueeze(1).broadcast_to([128, nh, 64])
            RT1, RT2 = RTs[slot]
            k1, k2 = ('RT1', slot), ('RT2', slot)
            tt('dve', RT1[:, 0:nh, :], src, Cv, ALU.mult, [sk, 'ropeC'], [k1])
            s4 = src.rearrange("p n (a h d) -> p n a h d", a=2, h=2)
            o4 = RT2[:, 0:nh, :].rearrange("p n (a h d) -> p n a h d", a=2, h=2)
            S4 = ropeS[:, c, :].rearrange("p (a h d) -> p a h d", a=2, h=2)
            for hh in range(2):
                for a in range(2):
                    tt('dve', o4[:, :, a, hh, :], s4[:, :, a, 1 - hh, :], S4[:, a, hh, :].unsqueeze(1).broadcast_to([128, nh, 16]), ALU.mult, [sk, 'ropeS'], [k2])
            tt('dve', dst, RT1[:, 0:nh, :], RT2[:, 0:nh, :], ALU.add, [k1, k2], [dk])

        RTs = [(A.alloc([128, 1, 64], F32), A.alloc([128, 1, 64], F32)) for _ in range(2)]

        def ev_kr(c, bk_, bkk):
            rope(bk_[:, 0:64].rearrange("p (n d) -> p n d", n=1), bkk, c, r1[:].rearrange("p (n d) -> p n d", n=1), 'r1', 1)
            dma('sp', D["o_kr"][l, c * 128:(c + 1) * 128, :], r1[:], ['r1'], [])
            s_, sk_ = stag_new()
            cp('pool', s_[:, 0:64], r1[:], ['r1'], [sk_])
            t_, tk = tbk()
            tr(t_[0:64, 0:128], s_[:, 0:64], [sk_], [tk])
            cp('act', kropeT[0:64, (2 + c) * 128:(3 + c) * 128], t_[0:64, 0:128], [tk], [('kropeT', 2 + c)])

        def ev_plain(col0, scale=1.0, func=None):
            def f(c, bk_, bkk):
                s_, sk_ = stag_new()
                if func is not None:
                    act(s_[:], bk_[:, :], func, [bkk], [sk_])
                elif scale != 1.0:
                    act(s_[:], bk_[:, :], AF.Copy, [bkk], [sk_], scale=scale)
                else:
                    cp('dve', s_[:], bk_[:, :], [bkk], [sk_])
                dma('sp', D["ztm"][c * 128:(c + 1) * 128, col0:col0 + 512], s_[:], [sk_], [('ztm', c, col0 // 512)])
            return f

        tm_pass(C_CQ, 512, lambda c, b, k: normed(c, b, k, gqb, 'gqb', cqnT, 'cqnT', 0, None))
        tm_pass(C_CKV, 512, lambda c, b, k: normed(c, b, k, gkvb, 'gkvb', ckvnT, 'ckvnT', 256, D["o_ckv"][l]))
        tm_pass(C_KR, 64, ev_kr)
        _stop('ZA')
        for i in range(2):
            tm_pass(C_VS + i * 512, 512, ev_plain(i * 512))
        for i in range(8):
            col = C_RQ + i * 512
            sc = (128 ** -0.5) if 2 <= i < 4 else 1.0
            tm_pass(col, 512, ev_plain(1024 + i * 512, scale=sc, func=AF.Silu if i >= 6 else None))

        def fm_pass(c0, func, row0):
            w_, wk = wt_new()
            dma('pool', w_[:], D["w_in"][l, :, c0:c0 + 512].rearrange("(kc p) n -> p kc n", p=128), [], [wk])
            for sub in range(4):
                for bi, (b0, bw) in enumerate(blocks_of(T)):
                    bk_, bkk = fb()
                    for kc in range(16):
                        mm(bk_[:, 0:bw], w_[:, kc, sub * 128:(sub + 1) * 128], hT[:, kc, b0:b0 + bw], kc == 0, kc == 15,
                           [wk] + [('hT', c) for c in range(b0 // 128, (b0 + bw) // 128)], [bkk])
                    s_, sk_ = stag_new()
                    if func is None:
                        cp('dve', s_[:, 0:bw], bk_[:, 0:bw], [bkk], [sk_])
                    else:
                        act(s_[:, 0:bw], bk_[:, 0:bw], func, [bkk], [sk_])
                    rr = row0 + sub * 128
                    dma('sp', D["zT"][rr:rr + 128, b0:b0 + bw], s_[:, 0:bw], [sk_], [('zT', rr // 128, bi)])
        for i in range(2):
            fm_pass(C_GPM + i * 512, AF.Silu, i * 512)
        for i in range(2):
            fm_pass(C_U + i * 512, None, 1024 + i * 512)
        for i in range(2):
            fm_pass(C_GPS + i * 512, AF.Silu, 2048 + i * 512)
        for i in range(12):
            fm_pass(C_MRG + i * 512, AF.Sigmoid, 3072 + i * 512)
        P.barrier()

        _stop('Z')
        A.off = att_mark
        wuq = A.alloc([128, 4, 1536], BF16)
        wukv = A.alloc([128, 4, 2048], BF16)
        qb = [A.alloc([128, 192], BF16) for _ in range(4)]
        qn = [A.alloc([128, T], BF16) for _ in range(2)]
        qr = [A.alloc([128, T], BF16) for _ in range(2)]
        knT = [A.alloc([128, TK], BF16) for _ in range(2)]
        V = [A.alloc([128, NK, 128], BF16) for _ in range(2)]
        PT = [A.alloc([128, 512], BF16) for _ in range(4)]
        of_ = [A.alloc([128, 512], F32) for _ in range(2)]
        rec = [A.alloc([128, 512], F32) for _ in range(2)]
        gt = [A.alloc([128, 512], BF16) for _ in range(2)]
        RTs = [(A.alloc([128, 1, 64], F32), A.alloc([128, 1, 64], F32)) for _ in range(2)]
        for n4 in range(3):
            dma('pool', wuq[:, :, n4 * 512:(n4 + 1) * 512], D["w_uq"][l, :, n4 * 512:(n4 + 1) * 512].rearrange("(kc p) n -> p kc n", p=128), [], ['wuq'])
        for n4 in range(4):
            dma('pool', wukv[:, :, n4 * 512:(n4 + 1) * 512], D["w_ukv"][l, :, n4 * 512:(n4 + 1) * 512].rearrange("(kc p) n -> p kc n", p=128), [], ['wukv'])
        for i in range(2):
            P.add('dve', lambda e, i=i: e.memset(qr[i][:], 0.0), [], [('qrm', i)] + [('qr', i, c) for c in range(NCH)])
            dma('pool', qr[i][64:96, :], D["qmask"][:, :], [], [('qrm', i)])
        wmA = [A.alloc([128, 16, 512], BF16) for _ in range(2)]
        mstA = [A.alloc([128, 512], F32) for _ in range(2)]
        ptc = {'i': 0}
        qblocks = blocks_of(T)
        kblocks = blocks_of(TK)
        _stop('A0')
        for h in range(8):
            hi = h % 2
            if l + 1 < NL and h < 6:
                mod_tile_load(l + 1, 2 * h, wmA)
                mod_tile_load(l + 1, 2 * h + 1, wmA)
            def qA(c):
                bk_, bkk = fb4()
                for kc in range(4):
                    mm(bk_[:, 0:192], cqnT[:, kc, c * 128:(c + 1) * 128], wuq[:, kc, h * 192:(h + 1) * 192], kc == 0, kc == 3, ['wuq', ('cqnT', c)], [bkk])
                q_, qk_ = qb[c % 4], ('qb', c % 4)
                cp('act', q_[:, 0:128], bk_[:, 0:128], [bkk], [qk_])
                rope(bk_[:, 128:192].rearrange("p (n d) -> p n d", n=1), bkk, c, q_[:, 128:192].rearrange("p (n d) -> p n d", n=1), qk_, 1, c % 2)

            def qB(c):
                q_, qk_ = qb[c % 4], ('qb', c % 4)
                t_, tk = tbk()
                tr(t_[:, 0:128], q_[:, 0:128], [qk_], [tk])
                tr(t_[0:64, 128:256], q_[:, 128:192], [qk_], [tk])
                cp('act', qn[hi][:, c * 128:(c + 1) * 128], t_[:, 0:128], [tk], [('qn', hi, c)])
                cp('act', qr[hi][0:64, c * 128:(c + 1) * 128], t_[0:64, 128:256], [tk], [('qr', hi, c)])
            qA(0)
            for c in range(NCH):
                if c + 1 < NCH:
                    qA(c + 1)
                qB(c)
            _stop('A1')
            for bi, (b0, bw) in enumerate(kblocks):
                bk_, bkk = fb4()
                for kc in range(4):
                    mm(bk_[:, 0:bw], wukv[:, kc, h * 256:h * 256 + 128], ckvnT[:, kc, b0:b0 + bw], kc == 0, kc == 3,
                       ['wukv'] + [('ckvnT', j) for j in range(b0 // 128, (b0 + bw) // 128)], [bkk])
                evcp(knT[hi][:, b0:b0 + bw], bk_[:, 0:bw], [bkk], [('knT', hi, bi)])
            _stop('A2')
            for j in range(NK):
                bk_, bkk = fb4()
                for kc in range(4):
                    mm(bk_[:, 0:128], ckvnT[:, kc, j * 128:(j + 1) * 128], wukv[:, kc, h * 256 + 128:h * 256 + 256], kc == 0, kc == 3,
                       ['wukv', ('ckvnT', j)], [bkk])
                evcp(V[hi][:, j, :], bk_[:, 0:128], [bkk], [('V', hi, j // 4)])
            _stop('A3')
            for bi, (b0, bw) in enumerate(qblocks):
                qres = [('qn', hi, c) for c in range(b0 // 128, (b0 + bw) // 128)] + [('qr', hi, c) for c in range(b0 // 128, (b0 + bw) // 128)] + [('qrm', hi)]
                Ob, Ok = pbf[4], ('pb', 4)
                Sb, Sk = pbf[5], ('pb', 5)
                i2 = bi % 2
                dma('sp', gt[i2][:, 0:bw], D["zT"][h * 128:(h + 1) * 128, b0:b0 + bw], [('zT', h, bi)], [('gt', i2)])

                def s_issue(j):
                    bk_, bkk = fb4()
                    mm(bk_[:, 0:bw], knT[hi][:, j * 128:(j + 1) * 128], qn[hi][:, b0:b0 + bw], True, False, [('knT', hi, j // 4)] + qres, [bkk])
                    mm(bk_[:, 0:bw], kropeT[:, j * 128:(j + 1) * 128], qr[hi][:, b0:b0 + bw], False, True, [('kropeT', j), 'kmaskrows'] + qres, [bkk])
                    return bk_, bkk
                LA = 2
                pend_s = [s_issue(j) for j in range(min(LA, NK))]
                for j in range(NK):
                    bk_, bkk = pend_s.pop(0)
                    ptc['i'] += 1
                    pi = ptc['i'] % 4
                    act(PT[pi][:, 0:bw], bk_[:, 0:bw], AF.Exp, [bkk], [('PT', pi)], scale=192 ** -0.5)
                    if j + LA < NK:
                        pend_s.append(s_issue(j + LA))
                    mm(Ob[:, 0:bw], V[hi][:, j, :], PT[pi][:, 0:bw], j == 0, j == NK - 1, [('V', hi, j // 4), ('PT', pi)], [Ok])
                    mm(Sb[:, 0:bw], ones[:, :], PT[pi][:, 0:bw], j == 0, j == NK - 1, ['ones', ('PT', pi)], [Sk])
                _stop('A4')
                recip(rec[i2][:, 0:bw], Sb[:, 0:bw], [Sk], [('rec', i2)])
                tt('dve', of_[i2][:, 0:bw], Ob[:, 0:bw], rec[i2][:, 0:bw], ALU.mult, [Ok, ('rec', i2)], [('of', i2)])
                s_, sk_ = stag_new()
                tt('pool', s_[:, 0:bw], of_[i2][:, 0:bw], gt[i2][:, 0:bw], ALU.mult, [('of', i2), ('gt', i2)], [sk_])
                dma('sp', D["gT"][0, h * 128:(h + 1) * 128, b0:b0 + bw], s_[:, 0:bw], [sk_], [('gT', 0, h, bi)])
            if l + 1 < NL and h < 6:
                mod_tile_compute(l + 1, 2 * h, wmA, mstA, fb4)
                mod_tile_compute(l + 1, 2 * h + 1, wmA, mstA, fb4)
        P.barrier()

        _stop('ATT')
        A.off = base_mark
        gsb = A.alloc([128, 1024], F32)
        bsT = A.alloc([128, 1024], F32)
        wsT = A.alloc([128, 1024], BF16)
        vs = [A.alloc([128, 1024], BF16) for _ in range(2)]
        vns = [A.alloc([128, 1024], BF16) for _ in range(2)]
        uT = [A.alloc([128, 8, 128], BF16) for _ in range(2)]
        gpT = [A.alloc([128, 8, 128], BF16) for _ in range(2)]
        t1s = [A.alloc([128, 1024], F32) for _ in range(2)]
        t2s = [A.alloc([128, 1024], F32) for _ in range(2)]
        go = [A.alloc([128, 8, 128], BF16) for _ in range(2)]
        dma('sp', gsb[:], D["g_sgu"][l:l + 1, :].partition_broadcast(128), [], ['gsb'])
        dma('sp', bsT[:], D["bsT"][l:l + 1, :].partition_broadcast(128), [], ['bsT'])
        dma('pool', wsT[:], D["wsT"][l], [], ['wsT'])
        zu = D["zT"][1024:2048, :].rearrange("(g p) t -> p g t", p=128)
        zg = D["zT"][2048:3072, :].rearrange("(g p) t -> p g t", p=128)
        gsg = D["gT"][1].rearrange("(g p) t -> p g t", p=128)
        def sgu_load(c):
            i = c % 2
            bi = (c * 128) // 512
            dma('sp', vs[i][:], D["ztm"][c * 128:(c + 1) * 128, 0:1024], [('ztm', c, 0), ('ztm', c, 1)], [('vs', i)])
            dma('sp', uT[i][:], zu[:, :, c * 128:(c + 1) * 128], [('zT', 8 + g, bi) for g in range(8)], [('uT', i)])
            dma('sp', gpT[i][:], zg[:, :, c * 128:(c + 1) * 128], [('zT', 16 + g, bi) for g in range(8)], [('gpT', i)])
        sgu_load(0)
        for c in range(NCH):
            i = c % 2
            if c + 1 < NCH:
                sgu_load(c + 1)
            vn, t1, t2 = vns[i], t1s[i], t2s[i]
            rs, sk = norm_stat(vs[i][:], ('vs', i), 1024, t1[:], ('t1', i))
            stt(vn[:], vs[i][:], rs, gsb[:], ALU.mult, ALU.mult, [('vs', i), sk, 'gsb'], [('vn', i)])
            for g4 in range(2):
                bk_, bkk = fb()
                for g in range(4):
                    gg = g4 * 4 + g
                    mm(bk_[:, g * 128:(g + 1) * 128], vn[:, gg * 128:(gg + 1) * 128], wsT[:, gg * 128:(gg + 1) * 128], True, True, [('vn', i), 'wsT'], [bkk])
                tt('dve', t2[:, g4 * 512:(g4 + 1) * 512], bk_[:, :], bsT[:, g4 * 512:(g4 + 1) * 512], ALU.add, [bkk, 'bsT'], [('t2', i)])
            tt('pool', t1[:], t2[:], uT[i][:].rearrange("p g t -> p (g t)"), ALU.mult, [('t2', i), ('uT', i)], [('t1', i)])
            tt('pool', go[i][:].rearrange("p g t -> p (g t)"), t1[:], gpT[i][:].rearrange("p g t -> p (g t)"), ALU.mult, [('t1', i), ('gpT', i)], [('go', i)])
            dma('sp', gsg[:, :, c * 128:(c + 1) * 128], go[i][:], [('go', i)], [('gT', 1, c)])
        P.barrier()

        _stop('SGU')
        A.off = base_mark
        grb = A.alloc([128, 1024], F32)
        lg = A.alloc([128, 16], F32)
        lt = A.alloc([128, 16], F32)
        Dt = A.alloc([128, 8, 128], F32)
        Dt2 = A.alloc([128, 8, 128], F32)
        xif = A.alloc([128, 8, 128], F32)
        xib = A.alloc([128, 8, 128], F32)
        zf = A.alloc([128, 8], F32)
        zb = A.alloc([128, 8], F32)
        gcv = A.alloc([128, 16], F32)
        zin = [A.alloc([128, 4096], BF16) for _ in range(2)]
        Rs = A.alloc([128, 8, 128], F32)
        Rin = A.alloc([128, 8, 128], F32)
        Rbf = [A.alloc([128, 8, 128], BF16) for _ in range(2)]
        Bn = [A.alloc([128, 8, 128], BF16) for _ in range(2)]
        D2 = lambda shape, dt: [A.alloc(shape, dt) for _ in range(2)]
        kzs, qTs, kTs, qxfs, qxbs, aTs, gros = (D2([128, 8, 128], BF16) for _ in range(7))
        of3s, cens, sq3s = (D2([128, 8, 128], F32) for _ in range(3))
        grT = [A.alloc([128, 8, 128], BF16) for _ in range(2)]
        dma('sp', grb[:], D["g_ret"][l:l + 1, :].partition_broadcast(128), [], ['grb'])
        dma('sp', lt[:], D["ret_decay"][l:l + 1, :].partition_broadcast(128), [], ['lt'])
        act(lt[:], lt[:], AF.Exp, ['lt'], ['lt'], scale=-1.0)
        act(lt[:], lt[:], AF.Ln, ['lt'], ['lt'], bias=1.0)
        ts(lg[:], lt[:], -1.0, None, ALU.mult, ALU.bypass, ['lt'], ['lg'])
        for h in range(8):
            act(Dt[:, h, :], dexf[:], AF.Exp, ['dexf', 'lg'], ['Dt'], scale=lg[:, h:h + 1])
            act(Dt2[:, h, :], dexb[:], AF.Exp, ['dexb', 'lg'], ['Dt2'], scale=lg[:, 8 + h:9 + h])
            act(xif[:, h, :], xiexp[:, 0:128], AF.Exp, ['xiexp', 'lg'], ['xif'], scale=lg[:, h:h + 1])
            act(xib[:, h, :], xiexp[:, 128:256], AF.Exp, ['xiexp', 'lg'], ['xib'], scale=lg[:, 8 + h:9 + h])
        tt('dve', Dt[:], Dt[:], Dt2[:], ALU.add, ['Dt', 'Dt2'], ['Dt'])
        act(zf[:], lg[:, 0:8], AF.Exp, ['lg', 'zexp'], ['zf'], scale=zexp[:, 0:1])
        act(zb[:], lg[:, 8:16], AF.Exp, ['lg', 'zexp'], ['zb'], scale=zexp[:, 1:2])
        act(gcv[:], lg[:], AF.Exp, ['lg'], ['gcv'], scale=128.0)

        def kz_make(zi, zk, zeta, zetak, i):
            k3 = zi[:, 1024:2048].rearrange("p (h d) -> p h d", h=8)
            tt('dve', kzs[i][:], k3, zeta[:].unsqueeze(2).broadcast_to([128, 8, 128]), ALU.mult, [zk, zetak], [('kz', i)])

        def state_update(zi, zk, gc, i):
            tt('pool', Rs[:], Rin[:], gc.unsqueeze(2).broadcast_to([128, 8, 128]), ALU.mult, ['Rin', 'gcv'], ['Rs'])
            for h4 in range(2):
                bk_, bkk = fb()
                for hh in range(4):
                    h = h4 * 4 + hh
                    mm(bk_[:, hh * 128:(hh + 1) * 128], kzs[i][:, h, :], zi[:, 2048 + h * 128:2048 + (h + 1) * 128], True, True, [('kz', i), zk], [bkk])
                tt('dve', Rs[:, h4 * 4:h4 * 4 + 4, :], Rs[:, h4 * 4:h4 * 4 + 4, :], bk_[:, :].rearrange("p (h e) -> p h e", h=4), ALU.add, ['Rs', bkk], ['Rs'])

        dma('sp', Rs[:], D["r0"][l, 1].rearrange("h d e -> d h e"), [], ['Rs'])
        def s1_load(n):
            dma('sp', zin[n % 2][:, 1024:3072], D["ztm"][n * 128:(n + 1) * 128, 2048:4096], [('ztm', n, 2 + q) for q in range(2, 6)], [('zin', n % 2)])
        s1_load(NCH - 1)
        for n in range(NCH - 1, -1, -1):
            i = n % 2
            if n - 1 >= 0:
                s1_load(n - 1)
            kz_make(zin[i], ('zin', i), zb, 'zb', i)
            ts(Rin[:], Rs[:], carry[:, NCH + n:NCH + n + 1], None, ALU.mult, ALU.bypass, ['Rs', 'carry'], ['Rin'])
            cp('pool', Rbf[i][:], Rin[:], ['Rin'], [('Rbf', i)])
            dma('sp', D["bst"][n].rearrange("p (h e) -> p h e", h=8), Rbf[i][:], [('Rbf', i)], [('bst', n)])
            state_update(zin[i], ('zin', i), gcv[:, 8:16], i)
            if n % 2 == 0:
                dma('sp', D["o_ret"][l, 1, n // 2].rearrange("h d e -> d h e"), Rs[:], ['Rs'], [])
        dma('sp', Rs[:], D["r0"][l, 0].rearrange("h d e -> d h e"), [], ['Rs'])
        grg = D["gT"][2].rearrange("(g p) t -> p g t", p=128)
        def s2_load(n):
            dma('sp', zin[n % 2][:], D["ztm"][n * 128:(n + 1) * 128, 1024:5120], [('ztm', n, 2 + q) for q in range(8)], [('zin', n % 2)])
            dma('sp', Bn[n % 2][:], D["bst"][n].rearrange("p (h e) -> p h e", h=8), [('bst', n)], [('Bn', n % 2)])
        s2_load(0)
        for n in range(NCH):
            i = n % 2
            zi, zk = zin[i], ('zin', i)
            qT, kT, qxf, qxb, aT, gro, of3, cen, sq3 = qTs[i], kTs[i], qxfs[i], qxbs[i], aTs[i], gros[i], of3s[i], cens[i], sq3s[i]
            if n + 1 < NCH:
                s2_load(n + 1)
            for which, dstT, dk in ((0, qT, ('qT', i)), (1, kT, ('kT', i))):
                for h4 in range(2):
                    t_, tk = tbk()
                    for hh in range(4):
                        h = h4 * 4 + hh
                        tr(t_[:, hh * 128:(hh + 1) * 128], zi[:, which * 1024 + h * 128:which * 1024 + (h + 1) * 128], [zk], [tk])
                    evcp(dstT[:, h4 * 4:h4 * 4 + 4, :], t_[:, 0:512].rearrange("p (h t) -> p h t", h=4), [tk], [dk])
            tt('dve', qxf[:], qT[:], xif[:], ALU.mult, [('qT', i), 'xif'], [('qxf', i)])
            tt('pool', qxb[:], qT[:], xib[:], ALU.mult, [('qT', i), 'xib'], [('qxb', i)])
            kz_make(zi, zk, zf, 'zf', i)
            for h4 in range(2):
                bk_, bkk = fb()
                for hh in range(4):
                    h = h4 * 4 + hh
                    mm(bk_[:, hh * 128:(hh + 1) * 128], kT[:, h, :], qT[:, h, :], True, True, [('kT', i), ('qT', i)], [bkk])
                tt('dve', aT[:, h4 * 4:h4 * 4 + 4, :], bk_[:, :].rearrange("p (h t) -> p h t", h=4), Dt[:, h4 * 4:h4 * 4 + 4, :], ALU.mult, [bkk, 'Dt'], [('aT', i)])
            ts(Rin[:], Rs[:], carry[:, n:n + 1], None, ALU.mult, ALU.bypass, ['Rs', 'carry'], ['Rin'])
            cp('pool', Rbf[i][:], Rin[:], ['Rin'], [('Rbf', i)])
            for h4 in range(2):
                bk_, bkk = fb()
                for hh in range(4):
                    h = h4 * 4 + hh
                    o_ = bk_[:, hh * 128:(hh + 1) * 128]
                    mm(o_, aT[:, h, :], zi[:, 2048 + h * 128:2048 + (h + 1) * 128], True, False, [('aT', i), zk], [bkk])
                    mm(o_, qxb[:, h, :], Bn[i][:, h, :], False, False, [('qxb', i), ('Bn', i)], [bkk])
                    mm(o_, qxf[:, h, :], Rbf[i][:, h, :], False, True, [('qxf', i), ('Rbf', i)], [bkk])
                cp('act', of3[:, h4 * 4:h4 * 4 + 4, :], bk_[:, :].rearrange("p (h e) -> p h e", h=4), [bkk], [('of3', i)])
            state_update(zi, zk, gcv[:, 0:8], i)
            if n % 2 == 1 or NCH == 1:
                dma('sp', D["o_ret"][l, 0, n // 2].rearrange("h d e -> d h e"), Rs[:], ['Rs'], [])
            st, sk = st_new()
            P.add('dve', lambda e, st=st, of3=of3: e.tensor_reduce(out=st[:, 0:8], in_=of3[:], axis=AX.X, op=ALU.add), [('of3', i)], [sk])
            stt(cen[:], st[:, 0:8].unsqueeze(2).broadcast_to([128, 8, 128]), -1.0 / 128, of3[:], ALU.mult, ALU.add, [sk, ('of3', i)], [('cen', i)])
            act(sq3[:], cen[:], AF.Square, [('cen', i)], [('sq3', i)])
            st2, sk2 = st_new()
            P.add('dve', lambda e, st2=st2, sq3=sq3: e.tensor_reduce(out=st2[:, 0:8], in_=sq3[:], axis=AX.X, op=ALU.add), [('sq3', i)], [sk2])
            act(st2[:, 0:8], st2[:, 0:8], AF.Sqrt, [sk2], [sk2], scale=1.0 / 128, bias=EPS)
            recip(st2[:, 0:8], st2[:, 0:8], [sk2], [sk2])
            tt('dve', cen[:], cen[:], st2[:, 0:8].unsqueeze(2).broadcast_to([128, 8, 128]), ALU.mult, [('cen', i), sk2], [('cen', i)])
            tt('pool', sq3[:].rearrange("p h e -> p (h e)"), cen[:].rearrange("p h e -> p (h e)"), grb[:], ALU.mult, [('cen', i), 'grb'], [('sq3', i)])
            tt('pool', gro[:].rearrange("p h e -> p (h e)"), sq3[:].rearrange("p h e -> p (h e)"), zi[:, 3072:4096], ALU.mult, [('sq3', i), zk], [('gro', i)])
            for h4 in range(2):
                t_, tk = tbk()
                for hh in range(4):
                    h = h4 * 4 + hh
                    tr(t_[:, hh * 128:(hh + 1) * 128], gro[:, h, :], [('gro', i)], [tk])
                evcp(grT[i][:, h4 * 4:h4 * 4 + 4, :], t_[:, 0:512].rearrange("p (h t) -> p h t", h=4), [tk], [('grT', i)])
            dma('sp', grg[:, :, n * 128:(n + 1) * 128], grT[i][:], [('grT', i)], [('gT', 2, n)])
        P.barrier()

        _stop('RET')
        A.off = base_mark
        wbr = [A.alloc([128, 8, DM], BF16) for _ in range(3)]
        gB = [[A.alloc([128, 8, 512], BF16) for _ in range(3)] for _ in range(2)]
        mB = [A.alloc([128, 3, 512], BF16) for _ in range(2)]
        tqs = [[A.alloc([128, 512], F32) for _ in range(3)] for _ in range(2)]
        for n4 in range(4):
            for j in range(3):
                dma('pool', wbr[j][:, :, n4 * 512:(n4 + 1) * 512], D["w_br"][l, j, :, n4 * 512:(n4 + 1) * 512].rearrange("(kc p) n -> p kc n", p=128), [], [('wbr', j, n4)])
        zm = D["zT"][3072:9216, :].rearrange("(j fc p) t -> fc p j t", j=3, p=128)
        def gB_load(bi):
            b0, bw = qblocks[bi]
            i = bi % 2
            gres = [[('gT', 0, h, bi) for h in range(8)], [('gT', 1, c) for c in range(b0 // 128, (b0 + bw) // 128)], [('gT', 2, c) for c in range(b0 // 128, (b0 + bw) // 128)]]
            for j in range(3):
                dma('sp', gB[i][j][:, :, 0:bw], D["gT"][j].rearrange("(kc p) t -> p kc t", p=128)[:, :, b0:b0 + bw], gres[j], [('gB', i, j)])

        def mB_load(it):
            bi, fc = divmod(it, 16)
            b0, bw = qblocks[bi]
            dma('sp', mB[fc % 2][:, :, 0:bw], zm[fc][:, :, b0:b0 + bw], [('zT', 24 + j * 16 + fc, bi) for j in range(3)], [('mB', fc % 2)])
        gB_load(0)
        mB_load(0)
        for bi, (b0, bw) in enumerate(qblocks):
            i = bi % 2
            if bi + 1 < len(qblocks):
                gB_load(bi + 1)
            for fc in range(16):
                m_, mk = mB[fc % 2], ('mB', fc % 2)
                tq, f2 = tqs[fc % 2], fc % 2
                if bi * 16 + fc + 1 < 16 * len(qblocks):
                    mB_load(bi * 16 + fc + 1)
                for j in range(3):
                    bk_, bkk = fb()
                    for kc in range(8):
                        mm(bk_[:, 0:bw], wbr[j][:, kc, fc * 128:(fc + 1) * 128], gB[i][j][:, kc, 0:bw], kc == 0, kc == 7, [('wbr', j, fc // 4), ('gB', i, j)], [bkk])
                    tt('dve', tq[j][:, 0:bw], bk_[:, 0:bw], m_[:, j, 0:bw], ALU.mult, [bkk, mk], [('tq', f2, j)])
                tt('pool', tq[0][:, 0:bw], tq[0][:, 0:bw], tq[1][:, 0:bw], ALU.add, [('tq', f2, 0), ('tq', f2, 1)], [('tq', f2, 0)])
                s_, sk_ = stag_new()
                tt('pool', s_[:, 0:bw], tq[0][:, 0:bw], tq[2][:, 0:bw], ALU.add, [('tq', f2, 0), ('tq', f2, 2)], [sk_])
                dma('sp', D["ysT"][fc * 128:(fc + 1) * 128, b0:b0 + bw], s_[:, 0:bw], [sk_], [('ysT', fc, bi)])
        P.barrier()

        _stop('OUT')
        A.off = base_mark
        wout = A.alloc([128, 16, DM], BF16)
        wt = [A.alloc([128, 16, 512], BF16) for _ in range(2)]
        Gb = A.alloc([128, DM], F32)
        ysB = [A.alloc([128, 16, 512], BF16) for _ in range(2)]
        xc = [A.alloc([128, DM], F32) for _ in range(2)]
        yos = [A.alloc([128, DM], F32) for _ in range(2)]
        yo = yos[0]
        junks = [A.alloc([128, 512], BF16) for _ in range(2)]
        dma('sp', Gb[:], D["b_mod"][l:l + 1, 2 * DM:3 * DM].partition_broadcast(128), [], ['Gb'])
        dma('sp', yos[1][:], D["modscr"][l, 2:3, :].partition_broadcast(128), [('modscr', l, 2, q) for q in range(4)], [('yo', 1)])
        tt('dve', Gb[:], Gb[:], yos[1][:], ALU.add, ['Gb', ('yo', 1)], ['Gb'])
        for n4 in range(4):
            dma('pool', wout[:, :, n4 * 512:(n4 + 1) * 512], D["w_out"][l, :, n4 * 512:(n4 + 1) * 512].rearrange("(kc p) n -> p kc n", p=128), [], [('wout', n4)])
        dma('sp', yo[:], D["g_post"][l:l + 1, :].partition_broadcast(128), [], [('yo', 0)])
        tt('dve', Gb[:], Gb[:], yo[:], ALU.mult, ['Gb', ('yo', 0)], ['Gb'])
        def ys_load(bi):
            b0, bw = qblocks[bi]
            dma('sp', ysB[bi % 2][:, :, 0:bw], D["ysT"].rearrange("(kc p) t -> p kc t", p=128)[:, :, b0:b0 + bw], [('ysT', fc, bi) for fc in range(16)], [('ysB', bi % 2)])

        def x_load(c):
            dma('sp', xc[c % 2][:], xsrc[c * 128:(c + 1) * 128, :], [(xkey, c)], [('xc', c % 2)])
        ys_load(0)
        x_load(0)
        for bi, (b0, bw) in enumerate(qblocks):
            i = bi % 2
            if bi + 1 < len(qblocks):
                ys_load(bi + 1)
            for cc in range(bw // 128):
                c = b0 // 128 + cc
                x_, xk = xc[c % 2], ('xc', c % 2)
                yo, yok = yos[c % 2], ('yo', c % 2)
                if c + 1 < NCH:
                    x_load(c + 1)
                st, sk = st_new()
                bks = []
                for nt in range(4):
                    bk_, bkk = fb()
                    bks.append((bk_, bkk))
                    for kc in range(16):
                        mm(bk_[:, :], ysB[i][:, kc, cc * 128:(cc + 1) * 128], wout[:, kc, nt * 512:(nt + 1) * 512], kc == 0, kc == 15, [('ysB', i), ('wout', nt)], [bkk])
                    act(junks[nt % 2][:], bk_[:, :], AF.Square, [bkk], [('junk', nt % 2), sk], accum=st[:, nt:nt + 1])
                    tt('dve', yo[:, nt * 512:(nt + 1) * 512], bk_[:, :], Gb[:, nt * 512:(nt + 1) * 512], ALU.mult, [bkk, 'Gb'], [yok])
                P.add('dve', lambda e, st=st: e.tensor_reduce(out=st[:, 4:5], in_=st[:, 0:4], axis=AX.X, op=ALU.add), [sk], [sk])
                act(st[:, 5:6], st[:, 4:5], AF.Sqrt, [sk], [sk], scale=1.0 / DM, bias=EPS)
                recip(st[:, 6:7], st[:, 5:6], [sk], [sk])
                stt(yo[:], yo[:], st[:, 6:7], x_[:], ALU.mult, ALU.add, [yok, sk, xk], [yok])
                dma('sp', D["y"][c * 128:(c + 1) * 128, :], yo[:], [yok, (xkey, c)], [('y', c)])
        P.barrier()
    try:
        for l in range(NL):
            layer(l)
    except StopBuild:
        P.barrier()
    P.emit(nc)
    return nc


def _consts(NCH, kind, nseq=0):
    T = NCH * 128
    TK = T + 256
    c = {}
    if kind == 'sample':
        pos = np.arange(T)
        row, col = (pos // 64).astype(np.float32), (pos % 64).astype(np.float32)
        inv = (10000.0 ** (-np.arange(16, dtype=np.float32) / 16)).astype(np.float32)
        ar, ac = row[:, None] * inv, col[:, None] * inv
        cr, sr, cc, sc = np.cos(ar), np.sin(ar), np.cos(ac), np.sin(ac)
        c["ropeC"] = np.concatenate([cr, cr, cc, cc], 1).astype(np.float32)
        c["ropeS"] = np.concatenate([-sr, sr, -sc, sc], 1).astype(np.float32)
        qm = np.zeros((32, T), np.float32); qm[0] = 1
        km = np.zeros((32, TK), np.float32)
        cf = np.ones(NCH, np.float32); cb = np.ones(NCH, np.float32)
    else:
        c["ropeC"] = np.ones((T, 64), np.float32)
        c["ropeS"] = np.zeros((T, 64), np.float32)
        qm = np.zeros((32, T), np.float32)
        km = np.full((32, TK), -BIG, np.float32)
        km[16:] = 0
        for s in range(T // 256):
            qm[s, s * 256:(s + 1) * 256] = 1
            km[s, 256 + s * 256:256 + (s + 1) * 256] = 0
        cf = np.array([0.0 if n % 2 == 0 else 1.0 for n in range(NCH)], np.float32)
        cb = np.array([0.0 if n % 2 == 1 else 1.0 for n in range(NCH)], np.float32)
    c["qmask"], c["kmask"] = qm, km
    c["carry"] = np.tile(np.concatenate([cf, cb])[None, :], (128, 1)).astype(np.float32)
    j = np.arange(128)[:, None].astype(np.float32); i = np.arange(128)[None, :].astype(np.float32)
    c["dexf"] = np.where(i >= j, i - j, 1e6).astype(np.float32)
    c["dexb"] = np.where(j > i, j - i, 1e6).astype(np.float32)
    c["xiexp"] = np.tile(np.concatenate([np.arange(128) + 1.0, 128.0 - np.arange(128)])[None, :], (128, 1)).astype(np.float32)
    c["zexp"] = np.stack([127.0 - np.arange(128), np.arange(128) * 1.0], 1).astype(np.float32)
    c["ident"] = np.eye(128, dtype=np.float32)
    return c


def _shared(inp, NL):
    f = lambda a: np.ascontiguousarray(a, dtype=np.float32)
    return {
        "w_mod": f(inp["w_mod"][:NL]), "b_mod": f(inp["b_mod"][:NL]), "g_pre": f(inp["g_pre"][:NL]), "g_post": f(inp["g_post"][:NL]),
        "w_in": f(inp["w_in"][:NL]), "g_q": f(inp["g_q"][:NL]), "g_kv": f(inp["g_kv"][:NL]), "w_uq": f(inp["w_uq"][:NL]),
        "w_ukv": f(inp["w_ukv"][:NL]), "g_sgu": f(inp["g_sgu"][:NL]),
        "wsT": f(np.transpose(inp["w_sgu"][:NL], (0, 3, 1, 2)).reshape(NL, 128, 1024)),
        "bsT": f(np.transpose(inp["b_sgu"][:NL], (0, 2, 1)).reshape(NL, 1024)),
        "ret_decay": f(inp["ret_decay"][:NL].reshape(NL, 16)), "g_ret": f(inp["g_ret"][:NL]),
        "w_br": f(np.stack([inp["w_br_mla"][:NL], inp["w_br_sgu"][:NL], inp["w_br_ret"][:NL]], 1)), "w_out": f(inp["w_out"][:NL]),
    }


def _core(x, cond, cckv, ckr, r0, NCH, kind):
    d = {"xin": np.ascontiguousarray(x, np.float32), "cond": np.ascontiguousarray(cond.reshape(16, 128).T, np.float32),
         "cckv": np.ascontiguousarray(cckv, np.float32), "ckr": np.ascontiguousarray(ckr, np.float32), "r0": np.ascontiguousarray(r0, np.float32)}
    d.update(_consts(NCH, kind))
    return d


_NC_CACHE = {}


def run_cores(cores, shared, NL, NCH):
    key = (NL, NCH)
    if key not in _NC_CACHE:
        _NC_CACHE[key] = build(NL, NCH)
    nc = _NC_CACHE[key]
    in_maps = [dict(c, **shared) for c in cores]
    res = run_bass_kernel_spmd(nc, in_maps, core_ids=list(range(len(cores))))
    return res.results


def kernel(**inp):
    NL, NCH = 4, 16
    shared = _shared(inp, NL)
    z = lambda *s: np.zeros(s, np.float32)
    act_ = []
    for b in range(2):
        act_.append(_core(inp["x_sample"][b], inp["c"][b], inp["cache_ckv"][b], inp["cache_krope"][b], inp["state_ret"][b], NCH, 'sample'))
    for p in range(2):
        act_.append(_core(inp["x_prompt"][p * 8:(p + 1) * 8].reshape(2048, 2048), inp["c_ctx"], z(NL, 256, 512), z(NL, 256, 64), z(NL, 2, 8, 128, 128), NCH, 'prompt'))
    idle = _core(z(2048, 2048), z(2048), z(NL, 256, 512), z(NL, 256, 64), z(NL, 2, 8, 128, 128), NCH, 'sample')
    slot = [0, 4, 1, 5]
    zshared = {k: np.zeros_like(v) for k, v in shared.items()}
    cores = [dict(idle, **zshared)] * 8
    for a, sl in zip(act_, slot):
        cores[sl] = dict(a, **shared)
    res_all = run_cores(cores, {}, NL, NCH)
    res = [res_all[sl] for sl in slot]
    y_sample = np.stack([res[0]["y"], res[1]["y"]], 0).astype(np.float32)
    y_prompt = np.concatenate([res[2]["y"].reshape(8, 256, 2048), res[3]["y"].reshape(8, 256, 2048)], 0).astype(np.float32)
    new_ckv = np.concatenate([np.transpose(res[2 + p]["o_ckv"].reshape(NL, 8, 256, 512), (1, 0, 2, 3)) for p in range(2)], 0).astype(np.float32)
    new_kr = np.concatenate([np.transpose(res[2 + p]["o_kr"].reshape(NL, 8, 256, 64), (1, 0, 2, 3)) for p in range(2)], 0).astype(np.float32)
    new_ret = np.concatenate([np.transpose(res[2 + p]["o_ret"], (2, 0, 1, 3, 4, 5)) for p in range(2)], 0).astype(np.float32)
    return (y_prompt, y_sample, new_ckv, new_kr, new_ret)
```
